# Optimizing a Trainium2 kernel written in Bass

```python
import math
import jax, jax.numpy as jnp
from jax import lax
import numpy as np

D_MODEL = 2048
BATCH = 2
SEQ = 8192
DEPTH = 1

CHUNK = 64
Q_BLOCK = 128
EPS = 1e-6

DA_HEADS = 8
DA_HEAD_DIM = 128
DA_V_DIM = 2 * DA_HEAD_DIM
DA_WIDTH = DA_HEADS * DA_V_DIM

MLA_HEADS = 16
MLA_Q_RANK = 768
MLA_KV_RANK = 512
MLA_NOPE_DIM = 128
MLA_ROPE_DIM = 64
MLA_V_DIM = 128
MLA_QK_DIM = MLA_NOPE_DIM + MLA_ROPE_DIM
MLA_WIDTH = MLA_HEADS * MLA_V_DIM
ROPE_THETA = 10000.0

D_FF = 5632

IN_SPLITS = (
    DA_HEADS * 2 * DA_HEAD_DIM,
    DA_HEADS * 2 * DA_HEAD_DIM,
    DA_WIDTH,
    MLA_Q_RANK,
    MLA_KV_RANK,
    MLA_ROPE_DIM,
    D_MODEL,
    D_MODEL,
)
N_IN = sum(IN_SPLITS)

kernel_name = 'hybrid_diffattn_mla_macaron'


def rmsnorm(x, g):
    xf = x.astype(jnp.float32)
    y = xf * lax.rsqrt(jnp.mean(xf * xf, axis=-1, keepdims=True) + EPS)
    return y.astype(x.dtype) * g


def swiglu(h, w1, w3, w2):
    return (jax.nn.silu(h @ w1) * (h @ w3)) @ w2


def chunk_mask(q0):
    t = q0 + jnp.arange(Q_BLOCK)
    s = jnp.arange(q0 + Q_BLOCK)
    return (s // CHUNK)[None, :] <= (t // CHUNK)[:, None]


def alibi_dist(q0):
    t = q0 + jnp.arange(Q_BLOCK)
    s = jnp.arange(q0 + Q_BLOCK)
    return jnp.abs(t[:, None] - s[None, :]).astype(jnp.float32)


def apply_rope(x):
    S, R = x.shape[1], x.shape[-1]
    inv = ROPE_THETA ** (-jnp.arange(0, R, 2, dtype=jnp.float32) / R)
    ang = jnp.arange(S, dtype=jnp.float32)[:, None] * inv[None, :]
    cos = jnp.cos(ang)[:, None, :].astype(x.dtype)
    sin = jnp.sin(ang)[:, None, :].astype(x.dtype)
    x1, x2 = x[..., : R // 2], x[..., R // 2:]
    return jnp.concatenate([x1 * cos - x2 * sin, x2 * cos + x1 * sin], axis=-1)


def diff_attention_core(q, k, v, lam):
    S = q.shape[1]
    scale = DA_HEAD_DIM ** -0.5
    slopes = 2.0 ** (-8.0 * jnp.arange(1, DA_HEADS + 1, dtype=jnp.float32) / DA_HEADS)
    outs = []
    for i in range(S // Q_BLOCK):
        q0 = i * Q_BLOCK
        kend = q0 + Q_BLOCK
        s = jnp.einsum('bqhmd,bkhmd->bhmqk', q[:, q0:kend], k[:, :kend],
                       preferred_element_type=jnp.float32) * scale
        s = s - slopes[None, :, None, None, None] * alibi_dist(q0)
        s = jnp.where(chunk_mask(q0), s, -jnp.inf)
        p = jax.nn.softmax(s, axis=-1)
        pd = p[:, :, 0] - lam * p[:, :, 1]
        outs.append(jnp.einsum('bhqk,bkhe->bqhe', pd, v[:, :kend].astype(jnp.float32)))
    return jnp.concatenate(outs, axis=1)


def mla_attention_core(q, k, v):
    S = q.shape[1]
    scale = MLA_QK_DIM ** -0.5
    outs = []
    for i in range(S // Q_BLOCK):
        q0 = i * Q_BLOCK
        kend = q0 + Q_BLOCK
        s = jnp.einsum('bqhd,bkhd->bhqk', q[:, q0:kend], k[:, :kend],
                       preferred_element_type=jnp.float32) * scale
        s = jnp.where(chunk_mask(q0), s, -jnp.inf)
        p = jax.nn.softmax(s, axis=-1)
        outs.append(jnp.einsum('bhqk,bkhe->bqhe', p, v[:, :kend].astype(jnp.float32)))
    return jnp.concatenate(outs, axis=1)


def diff_attention_branch(qa, ka, va, q_norm_g, k_norm_g, lq1, lk1, lq2, lk2, subln_g, lambda_init):
    B, S = qa.shape[:2]
    q = rmsnorm(qa.reshape(B, S, DA_HEADS, 2, DA_HEAD_DIM), q_norm_g)
    k = rmsnorm(ka.reshape(B, S, DA_HEADS, 2, DA_HEAD_DIM), k_norm_g)
    v = va.reshape(B, S, DA_HEADS, DA_V_DIM)
    f32 = jnp.float32
    lam = (jnp.exp(jnp.sum(lq1.astype(f32) * lk1.astype(f32)))
           - jnp.exp(jnp.sum(lq2.astype(f32) * lk2.astype(f32))) + lambda_init)
    o = diff_attention_core(q, k, v, lam)
    o = rmsnorm(o, subln_g) * (1.0 - lambda_init)
    return o.astype(qa.dtype).reshape(B, S, DA_WIDTH)


def mla_branch(cq, ckv, k_rope, q_a_norm_g, w_qb, kv_a_norm_g, w_kvb, q_norm_g, k_norm_g):
    B, S = cq.shape[:2]
    q = (rmsnorm(cq, q_a_norm_g) @ w_qb).reshape(B, S, MLA_HEADS, MLA_QK_DIM)
    kv = (rmsnorm(ckv, kv_a_norm_g) @ w_kvb).reshape(B, S, MLA_HEADS, MLA_NOPE_DIM + MLA_V_DIM)
    k_nope, v = kv[..., :MLA_NOPE_DIM], kv[..., MLA_NOPE_DIM:]
    k_rope_h = jnp.broadcast_to(k_rope[:, :, None, :], (B, S, MLA_HEADS, MLA_ROPE_DIM))
    k = jnp.concatenate([k_nope, k_rope_h], axis=-1)
    q = rmsnorm(q, q_norm_g)
    k = rmsnorm(k, k_norm_g)
    q = jnp.concatenate([q[..., :MLA_NOPE_DIM], apply_rope(q[..., MLA_NOPE_DIM:])], axis=-1)
    k = jnp.concatenate([k[..., :MLA_NOPE_DIM], apply_rope(k[..., MLA_NOPE_DIM:])], axis=-1)
    o = mla_attention_core(q, k, v)
    return o.astype(cq.dtype).reshape(B, S, MLA_WIDTH)


def setup_inputs(seed: int = 0) -> dict:
    key = jax.random.key(seed)
    ks = list(jax.random.split(key, 32))
    f32 = jnp.float32

    def w(shape, fan_in):
        return jax.random.normal(ks.pop(), shape, f32) * fan_in ** -0.5

    def g(shape):
        return 1.0 + 0.02 * jax.random.normal(ks.pop(), shape, f32)

    def lam():
        return 0.1 * jax.random.normal(ks.pop(), (DEPTH, DA_HEAD_DIM), f32)

    L = DEPTH
    return {
        'x': jax.random.normal(ks.pop(), (BATCH, SEQ, D_MODEL), f32),
        'ffn1_norm_g': g((L, D_MODEL)),
        'ffn1_w1': w((L, D_MODEL, D_FF), D_MODEL),
        'ffn1_w3': w((L, D_MODEL, D_FF), D_MODEL),
        'ffn1_w2': w((L, D_FF, D_MODEL), D_FF),
        'mix_norm_g': g((L, D_MODEL)),
        'w_in': w((L, D_MODEL, N_IN), D_MODEL),
        'da_q_norm_g': g((L, DA_HEAD_DIM)),
        'da_k_norm_g': g((L, DA_HEAD_DIM)),
        'da_lambda_q1': lam(),
        'da_lambda_k1': lam(),
        'da_lambda_q2': lam(),
        'da_lambda_k2': lam(),
        'da_subln_g': g((L, DA_V_DIM)),
        'mla_q_a_norm_g': g((L, MLA_Q_RANK)),
        'mla_w_qb': w((L, MLA_Q_RANK, MLA_HEADS * MLA_QK_DIM), MLA_Q_RANK),
        'mla_kv_a_norm_g': g((L, MLA_KV_RANK)),
        'mla_w_kvb': w((L, MLA_KV_RANK, MLA_HEADS * (MLA_NOPE_DIM + MLA_V_DIM)), MLA_KV_RANK),
        'mla_q_norm_g': g((L, MLA_QK_DIM)),
        'mla_k_norm_g': g((L, MLA_QK_DIM)),
        'w_branch_a': w((L, DA_WIDTH, D_MODEL), DA_WIDTH),
        'w_branch_b': w((L, MLA_WIDTH, D_MODEL), MLA_WIDTH),
        'w_out': w((L, D_MODEL, D_MODEL), D_MODEL),
        'ffn2_norm_g': g((L, D_MODEL)),
        'ffn2_w1': w((L, D_MODEL, D_FF), D_MODEL),
        'ffn2_w3': w((L, D_MODEL, D_FF), D_MODEL),
        'ffn2_w2': w((L, D_FF, D_MODEL), D_FF),
    }


def reference(x, ffn1_norm_g, ffn1_w1, ffn1_w3, ffn1_w2, mix_norm_g, w_in,
              da_q_norm_g, da_k_norm_g, da_lambda_q1, da_lambda_k1, da_lambda_q2, da_lambda_k2,
              da_subln_g, mla_q_a_norm_g, mla_w_qb, mla_kv_a_norm_g, mla_w_kvb,
              mla_q_norm_g, mla_k_norm_g, w_branch_a, w_branch_b, w_out,
              ffn2_norm_g, ffn2_w1, ffn2_w3, ffn2_w2):
    offsets = [int(o) for o in np.cumsum(IN_SPLITS)[:-1]]
    for l in range(DEPTH):
        lambda_init = 0.8 - 0.6 * math.exp(-0.3 * l)
        x = x + 0.5 * swiglu(rmsnorm(x, ffn1_norm_g[l]), ffn1_w1[l], ffn1_w3[l], ffn1_w2[l])
        h = rmsnorm(x, mix_norm_g[l])
        proj = h @ w_in[l]
        qa, ka, va, cq, ckv, k_rope, ga, gb = jnp.split(proj, offsets, axis=-1)
        y_a = diff_attention_branch(qa, ka, va, da_q_norm_g[l], da_k_norm_g[l],
                                    da_lambda_q1[l], da_lambda_k1[l], da_lambda_q2[l], da_lambda_k2[l],
                                    da_subln_g[l], lambda_init)
        y_b = mla_branch(cq, ckv, k_rope, mla_q_a_norm_g[l], mla_w_qb[l], mla_kv_a_norm_g[l],
                         mla_w_kvb[l], mla_q_norm_g[l], mla_k_norm_g[l])
        merged = jax.nn.sigmoid(ga) * (y_a @ w_branch_a[l]) + jax.nn.sigmoid(gb) * (y_b @ w_branch_b[l])
        x = x + merged @ w_out[l]
        x = x + 0.5 * swiglu(rmsnorm(x, ffn2_norm_g[l]), ffn2_w1[l], ffn2_w3[l], ffn2_w2[l])
    return x
```

```python
import math
from contextlib import ExitStack

import numpy as np
import concourse.bass as bass
import concourse.mybir as mybir
from concourse.bass_utils import run_bass_kernel_spmd

F32 = mybir.dt.float32
BF16 = mybir.dt.bfloat16
AF = mybir.ActivationFunctionType
ALU = mybir.AluOpType

NCORES = 8
D = 2048
DFF = 5632
NF = DFF // 128
KC = D // 128
SEQ = 8192
NTOK = 2048
TT = 512
NSLOT = 4
EPS = 1e-6
LAMBDA_INIT = 0.8 - 0.6 * math.exp(-0.3 * 0)
NEG = -1.0e6

ENGS = ("pe", "act", "dve", "pool", "sp")
NDMASEM = {"sp": 40, "pool": 24, "act": 8}


class T:
    __slots__ = ("w", "r", "rd")

    def __init__(self):
        self.w = None
        self.r = {}
        self.rd = []


class Buf:
    __slots__ = ("ap", "ts")

    def __init__(self, ap, ts):
        self.ap = ap
        self.ts = ts

    def __getitem__(self, idx):
        return Buf(self.ap[idx], self.ts)

    def v(self, ap):
        return Buf(ap, self.ts)


class Ins:
    __slots__ = ("eng", "fn", "waits", "signal", "sem", "val", "kind", "prev")

    def __init__(self, eng, fn, kind):
        self.eng = eng
        self.fn = fn
        self.kind = kind
        self.waits = []
        self.signal = False
        self.sem = None
        self.val = 0
        self.prev = None


class Prog:
    def __init__(self):
        self.streams = {e: [] for e in ENGS}

    def emit(self, eng, fn, reads=(), writes=(), kind="c"):
        ins = Ins(eng, fn, kind)
        deps = {}
        true_deps = set()
        for b in reads:
            for t in b.ts:
                if t.w is not None:
                    deps[id(t.w)] = t.w
                    true_deps.add(id(t.w))
        for b in writes:
            for t in b.ts:
                if t.w is not None:
                    deps[id(t.w)] = t.w
                    true_deps.add(id(t.w))
                for r in t.r.values():
                    deps[id(r)] = r
                for r in t.rd:
                    deps[id(r)] = r
        for d in deps.values():
            if d.kind == "c" and d.eng == eng and kind == "c":
                if eng == "pe" or id(d) not in true_deps:
                    continue
            d.signal = True
            ins.waits.append(d)
        for b in reads:
            for t in b.ts:
                if kind == "c":
                    t.r[eng] = ins
                else:
                    t.rd.append(ins)
        for b in writes:
            for t in b.ts:
                t.w = ins
                t.r = {}
                t.rd = []
        self.streams[eng].append(ins)
        return ins

    def finalize(self):
        sems = set()
        for e in ENGS:
            cnt, semi, k, ncc = 0, 0, 0, 0
            P = NDMASEM.get(e, 1)
            for ins in self.streams[e]:
                if ins.kind == "c":
                    if ins.signal:
                        cnt += 1
                        if cnt > 30000:
                            semi += 1
                            cnt = 1
                        ins.sem = ("c", e, semi)
                        ins.val = cnt
                        sems.add(ins.sem)
                elif ins.kind == "d":
                    ins.sem = ("d", e, k % P)
                    ins.val = 16 * (k // P + 1)
                    if k >= P:
                        ins.prev = (ins.sem, 16 * (k // P))
                    k += 1
                    sems.add(ins.sem)
                else:
                    ins.sem = ("cc", e, ncc)
                    ins.val = 1
                    ncc += 1
                    sems.add(ins.sem)
        return sorted(sems)

    def replay(self, eng, e, semh, final_waits=None):
        waited = {}

        def w(sem, val):
            if waited.get(sem, 0) < val:
                e.wait_ge(semh[sem], val)
                waited[sem] = val

        for ins in self.streams[eng]:
            for d in ins.waits:
                w(d.sem, d.val)
            if ins.prev is not None:
                w(*ins.prev)
            r = ins.fn(e)
            if ins.kind == "d":
                r.then_inc(semh[ins.sem], 16)
            elif ins.kind == "cc":
                r.then_inc(semh[ins.sem], 1)
            elif ins.signal:
                r.then_inc(semh[ins.sem], 1)
        if final_waits:
            for sem, val in final_waits:
                w(sem, val)


class Builder:
    def __init__(self, stage):
        self.stage = stage
        self.nc = bass.Bass("TRN2", target_bir_lowering=False)
        self.p = Prog()
        self.es = ExitStack()
        self.sb_off = 0
        self.dram = {}

    def dram_in(self, name, shape, dt=F32):
        h = self.nc.dram_tensor(name, list(shape), dt, kind="ExternalInput")
        b = Buf(h.ap(), [T()])
        self.dram[name] = b
        return b

    def dram_out(self, name, shape, dt=F32):
        h = self.nc.dram_tensor(name, list(shape), dt, kind="ExternalOutput")
        b = Buf(h.ap(), [T()])
        self.dram[name] = b
        return b

    def dram_tmp(self, name, shape, dt=BF16):
        h = self.nc.dram_tensor(name, list(shape), dt)
        return h

    def sbuf_init(self, nbytes):
        self.arena = self.es.enter_context(self.nc.sbuf_tensor("arena", [128, nbytes // 4], F32))
        self.arena_bytes = nbytes
        self.pages = [T() for _ in range(nbytes // 512)]
        self.psum = self.es.enter_context(self.nc.psum_tensor("psum", [128, 8 * 512], F32))
        self.pspages = [T() for _ in range(8 * 4)]

    def sb(self, off, dt, free_shape, parts=128):
        esz = 4 if dt == F32 else 2
        n = int(np.prod(free_shape))
        assert off % 4 == 0 and off + n * esz <= self.arena_bytes, (off, n, esz)
        base = self.arena.bitcast(dt) if dt != F32 else self.arena
        ap = base[0:parts, off // esz: off // esz + n]
        if len(free_shape) == 2:
            ap = ap.rearrange("p (a b) -> p a b", a=free_shape[0])
        elif len(free_shape) == 3:
            ap = ap.rearrange("p (a b c) -> p a b c", a=free_shape[0], b=free_shape[1])
        ts = self.pages[off // 512: (off + n * esz + 511) // 512]
        return Buf(ap, ts)

    def ps(self, bank, cols=512, parts=128, coff=0):
        ap = self.psum[0:parts, bank * 512 + coff: bank * 512 + coff + cols]
        ts = self.pspages[bank * 4 + coff // 128: bank * 4 + (coff + cols + 127) // 128]
        return Buf(ap, ts)

    def dma(self, out, in_, eng="sp"):
        o, i = out.ap, in_.ap
        return self.p.emit(eng, lambda e: e.dma_start(out=o, in_=i), reads=[in_], writes=[out], kind="d")

    def mm(self, out, lhsT, rhs, start, stop):
        o, l, r = out.ap, lhsT.ap, rhs.ap
        return self.p.emit("pe", lambda e: e.matmul(o, l, r, start=start, stop=stop, skip_group_check=True),
                           reads=[lhsT, rhs], writes=[out])

    def act(self, out, in_, func, bias=None, scale=None, eng="act"):
        o, i = out.ap, in_.ap
        kw = {}
        rd = [in_]
        if bias is not None:
            if isinstance(bias, Buf):
                kw["bias"] = bias.ap
                rd.append(bias)
            else:
                kw["bias"] = bias
        if scale is not None:
            if isinstance(scale, Buf):
                kw["scale"] = scale.ap
                rd.append(scale)
            else:
                kw["scale"] = scale
        return self.p.emit(eng, lambda e: e.activation(out=o, in_=i, func=func, **kw), reads=rd, writes=[out])

    def tt(self, out, a, b, op, eng="dve"):
        o, x, y = out.ap, a.ap, b.ap
        return self.p.emit(eng, lambda e: e.tensor_tensor(out=o, in0=x, in1=y, op=op), reads=[a, b], writes=[out])

    def stt(self, out, in0, scalar, in1, op0, op1, eng="dve"):
        o, x, y = out.ap, in0.ap, in1.ap
        rd = [in0, in1]
        if isinstance(scalar, Buf):
            s = scalar.ap
            rd.append(scalar)
        else:
            s = scalar
        return self.p.emit(eng, lambda e: e.scalar_tensor_tensor(out=o, in0=x, scalar=s, in1=y, op0=op0, op1=op1),
                           reads=rd, writes=[out])

    def ts(self, out, in0, s1, op0, s2=None, op1=None, eng="dve"):
        o, x = out.ap, in0.ap
        rd = [in0]
        a1 = s1
        if isinstance(s1, Buf):
            a1 = s1.ap
            rd.append(s1)
        a2 = s2
        if isinstance(s2, Buf):
            a2 = s2.ap
            rd.append(s2)
        if op1 is None:
            return self.p.emit(eng, lambda e: e.tensor_scalar(out=o, in0=x, scalar1=a1, scalar2=None, op0=op0),
                               reads=rd, writes=[out])
        return self.p.emit(eng, lambda e: e.tensor_scalar(out=o, in0=x, scalar1=a1, scalar2=a2, op0=op0, op1=op1),
                           reads=rd, writes=[out])

    def recip(self, out, in_, eng="dve"):
        o, i = out.ap, in_.ap
        return self.p.emit(eng, lambda e: e.reciprocal(out=o, in_=i), reads=[in_], writes=[out])

    def copy(self, out, in_, eng="dve"):
        o, i = out.ap, in_.ap
        return self.p.emit(eng, lambda e: e.tensor_copy(out=o, in_=i), reads=[in_], writes=[out])

    def memset(self, out, val, eng="dve"):
        o = out.ap
        return self.p.emit(eng, lambda e: e.memset(o, val), reads=[], writes=[out])


C_ONES, C_RMAT, C_GF1, C_GMIX, C_GF2, C_GQA, C_GKVA = 0, 128, 192, 208, 224, 240, 246
C_GQDA, C_GKDA, C_GQMN, C_GQMR, C_GKMN, C_GKMR, C_SUB0, C_SUB1, C_EPS, C_LAM = 250, 251, 252, 253, 254, 255, 256, 257, 258, 259
C_A2, C_AC, C_MM, CST_W = 264, 328, 1352, 2376

WIN_FM = [("q", 0, 16, 128), ("k", 2048, 16, 128), ("cq", 6144, 6, 128), ("ckv", 6912, 4, 128),
          ("kr", 7424, 1, 64), ("ga", 7488, 16, 128), ("gb", 9536, 16, 128)]
WIN_V0 = 4096


C_ONES, C_RMAT, C_GF1, C_GMIX, C_GF2, C_GQA, C_GKVA = 0, 128, 192, 208, 224, 240, 246
C_GQDA, C_GKDA, C_GQMN, C_GQMR, C_GKMN, C_GKMR, C_SUB0, C_SUB1, C_EPS, C_LAM = 250, 251, 252, 253, 254, 255, 256, 257, 258, 259
C_A2, C_AC, C_MM, CST_W = 264, 328, 1352, 2376
C_GQMRW = 263

WG = [("f1w1", 48, 2048), ("f1w3", 48, 2048), ("f1w2", 16, 5632), ("win", 80, 2048), ("winv", 8, 4096),
      ("wqb", 16, 1536), ("wkvbk", 16, 512), ("wkvbv", 8, 1024), ("wa", 16, 2048), ("wb", 16, 2048),
      ("wo", 16, 2048), ("f2w1", 48, 2048), ("f2w3", 48, 2048), ("f2w2", 16, 5632)]
WGD = {n: (t, e) for n, t, e in WG}
T_Q, T_K, T_CQ, T_CKV, T_KR, T_GA, T_GB = 0, 16, 32, 38, 42, 43, 59
GROWS = 2304
R_KDA, R_VDA, R_KMN, R_KMR, R_VM = 0, 512, 1024, 1536, 1792
SLOPES = [2.0 ** (-8.0 * (h + 1) / 8) for h in range(8)]


def kpos(m):
    return 32 * (m // 32) + 4 * (m % 8) + (m % 32) // 8


class Kern(Builder):
    def dT(self, key):
        if key not in self.dram:
            self.dram[key] = T()
        return self.dram[key]

    def wtile(self, name, t, m):
        h = self.wg[name]
        ap = h[t * 128:(t + 1) * 128, :].rearrange("p (a b) -> p a b", b=m)
        return Buf(ap, [self.dT(("wg", name))])

    def dv(self, h, key, idx):
        return Buf(h[idx], [self.dT((h.name, key))])

    def gview(self, hnd, rank, row0, nrows, pattern, key, **sizes):
        tot = int(np.prod(hnd.shape))
        flat = hnd.reshape([tot])
        a = (rank * GROWS + row0) * 2048
        ap = flat[a:a + nrows * 2048].rearrange(pattern, **sizes)
        return Buf(ap, [self.dT((hnd.name, key))])

    def setup(self):
        nc = self.nc
        self.sbuf_init(186368)
        I = self.dram_in
        self.xT = I("xT", [D, NTOK])
        self.cst_d = I("cst", [128, CST_W])
        self.rope_d = I("rope", [64, 3, NTOK])
        tmp = self.dram_tmp
        self.wsrc, self.wown, self.wg = {}, {}, {}
        for n, nt, e in WG:
            self.wsrc[n] = I("ws_" + n, [nt // 8 * 128, e])
            self.wown[n] = tmp("wo_" + n, [nt // 8 * 128, e])
            self.wg[n] = tmp("wg_" + n, [nt * 128, e])
        self.out_d = self.dram_out("outT", [D, NTOK])
        self.x1_d = tmp("x1T", [KC, 128, NTOK], F32)
        self.qda_d = tmp("qda", [16, 128, NTOK])
        self.qmn_d = tmp("qmn", [16, 128, NTOK])
        self.qmr_d = tmp("qmr", [16, 64, NTOK])
        self.gT_d = tmp("gT", [32, 128, NTOK])
        self.yT_d = tmp("yT", [32, 128, NTOK])
        self.gin = [tmp("gin%d" % s, [GROWS, 2048]) for s in range(NSLOT)]
        self.gout = [tmp("gout%d" % s, [8 * GROWS, 2048]) for s in range(NSLOT)]
        sb = self.sb
        self.cst = sb(0, F32, [CST_W])
        self.ones_f = self.cst[:, C_ONES:C_ONES + 128]
        self.rmat = self.cst[0:64, C_RMAT:C_RMAT + 64]
        self.eps = self.cst[:, C_EPS:C_EPS + 1]
        self.ac = self.cst[:, C_AC:C_AC + 1024]
        self.mmk = self.cst[:, C_MM:C_MM + 1024]
        self.ones_b = sb(9728, BF16, [128])
        self.small = sb(10240, F32, [64])
        self.btab = sb(10496, F32, [8, 64])
        self.cos2 = sb(12544, F32, [TT], 64)
        self.sin2 = sb(14592, F32, [TT], 64)
        self.sinS = sb(182784, F32, [TT], 64)
        B0 = 16896
        self.B0 = B0
        self.xres = [sb(B0 + 2048 * k, F32, [TT]) for k in range(KC)]
        B1 = B0 + 32768
        self.hT = [sb(B1 + 1024 * k, BF16, [TT]) for k in range(KC)]
        B2 = B1 + 16384
        self.B2 = B2
        self.uT = [sb(B2 + 1024 * k, BF16, [TT]) for k in range(NF)]
        B3 = B2 + 45056
        self.B3 = B3
        self.wb = [sb(B3 + 4096 * k, BF16, [KC, 128]) for k in range(6)]
        B4 = B3 + 24576
        self.B4 = B4
        self.w2b = [sb(B4 + 11264 * k, BF16, [NF, 128]) for k in range(2)]
        B5 = B4 + 22528
        self.f32s = [sb(B5 + 2048 * k, F32, [TT]) for k in range(9)]
        self.sq = self.f32s[0:2]
        self.sq16 = [sb(B5 + 2048 * k, BF16, [TT]) for k in range(5)]
        self.rstd = self.f32s[2]
        self.sil = self.f32s[3:5]
        self.ob = [sb(B5 + 18432 + 1024 * k, BF16, [TT]) for k in range(6)]
        self.vb = [sb(B5 + 18432 + 2048 * k, BF16, [4, 256]) for k in range(3)]
        assert B5 + 24576 <= self.arena_bytes, B5 + 24576
        self.cnt = {}

    def rr(self, name, n):
        v = self.cnt.get(name, 0)
        self.cnt[name] = v + 1
        return v % n

    def wbv(self, kc, m):
        i = self.rr("wb", 6)
        return self.sb(self.B3 + 4096 * i, BF16, [kc, m])

    def consts(self):
        self.dma(self.cst, self.cst_d)
        self.memset(self.ones_b, 1.0)
        sm = self.small
        c = self.cst
        self.ts(sm[:, 0:1], c[:, C_GQDA:C_GQDA + 1], 128.0 ** -0.5, ALU.mult)
        self.ts(sm[:, 1:2], c[:, C_GQMN:C_GQMN + 1], 192.0 ** -0.5, ALU.mult)
        self.ts(sm[:, 2:3], c[:, C_GQMR:C_GQMR + 1], 192.0 ** -0.5, ALU.mult)
        self.ts(sm[:, 6:7], c[:, C_GQMRW:C_GQMRW + 1], 192.0 ** -0.5, ALU.mult)
        self.ts(sm[:, 3:4], c[:, C_SUB0:C_SUB0 + 1], 1.0 - LAMBDA_INIT, ALU.mult)
        self.ts(sm[:, 4:5], c[:, C_SUB1:C_SUB1 + 1], 1.0 - LAMBDA_INIT, ALU.mult)
        self.tt(sm[:, 8:9], c[:, C_LAM:C_LAM + 1], c[:, C_LAM + 1:C_LAM + 2], ALU.mult)
        self.tt(sm[:, 9:10], c[:, C_LAM + 2:C_LAM + 3], c[:, C_LAM + 3:C_LAM + 4], ALU.mult)
        ps = self.ps(7, 2)
        self.mm(ps, self.ones_f, sm[:, 8:10], True, True)
        self.act(sm[:, 10:12], ps, AF.Exp)
        self.tt(sm[:, 12:13], sm[:, 11:12], sm[:, 10:11], ALU.subtract)
        self.ts(sm[:, 5:6], sm[:, 12:13], -LAMBDA_INIT, ALU.add)
        for h in range(8):
            self.ts(self.btab[:, h, :], c[:, C_A2:C_A2 + 64], SLOPES[h], ALU.mult)

    def convert(self, names):
        for n in names:
            nt, e = WGD[n]
            for t in range(nt // 8):
                sl = slice(t * 128, (t + 1) * 128)
                src = Buf(self.wsrc[n].ap[sl, :], self.wsrc[n].ts)
                dst = Buf(self.wown[n][sl, :], [self.dT(("wown", n))])
                self.dma(dst, src, eng="pool")

    def gather_w(self, names):
        for n in names:
            src = Buf(self.wown[n].ap(), [self.dT(("wown", n))])
            dst = Buf(self.wg[n].ap(), [self.dT(("wg", n))])
            self.allgather(dst, src)

    def allgather(self, dst, src):
        o, i = dst.ap.opt(), src.ap.opt()
        self.p.emit("pool", lambda e: e.collective_compute("AllGather", ALU.bypass, replica_groups=[list(range(NCORES))],
                                                           ins=[i], outs=[o]),
                    reads=[src], writes=[dst], kind="cc")

    def rmsnorm(self, gcol, bank):
        ps = self.ps(bank)
        for kc in range(KC):
            sq = self.sq16[kc % 2]
            self.act(sq, self.xres[kc], AF.Square)
            self.mm(ps, self.ones_b, sq, kc == 0, kc == KC - 1)
        self.act(self.rstd, ps, AF.Sqrt, bias=self.eps, scale=1.0 / D)
        self.recip(self.rstd, self.rstd)
        for kc in range(KC):
            self.stt(self.hT[kc], self.xres[kc], self.cst[:, gcol + kc:gcol + kc + 1], self.rstd, ALU.mult, ALU.mult)

    def ffn(self, k):
        for f in range(NF):
            b1, b3 = self.wbv(KC, 128), self.wbv(KC, 128)
            self.dma(b1, self.wtile(k + "w1", f, 128))
            self.dma(b3, self.wtile(k + "w3", f, 128))
            pA, pB = self.ps((f % 2) * 2), self.ps((f % 2) * 2 + 1)
            for kc in range(KC):
                self.mm(pA, b1[:, kc, :], self.hT[kc], kc == 0, kc == KC - 1)
            for kc in range(KC):
                self.mm(pB, b3[:, kc, :], self.hT[kc], kc == 0, kc == KC - 1)
            sil = self.sil[f % 2]
            self.act(sil, pA, AF.Silu)
            self.tt(self.uT[f], sil, pB, ALU.mult)
        for dc in range(KC):
            wb2 = self.w2b[dc % 2]
            self.dma(wb2, self.wtile(k + "w2", dc, 128))
            pO = self.ps(4 + dc % 2)
            for f in range(NF):
                self.mm(pO, wb2[:, f, :], self.uT[f], f == 0, f == NF - 1)
            self.stt(self.xres[dc], pO, 0.5, self.xres[dc], ALU.mult, ALU.add)

    def pipe(self, n, s1, s2):
        for i in range(n):
            s1(i)
            if i > 0:
                s2(i - 1)
        s2(n - 1)

    def rstd_from(self, ps2, ndim, parts=128):
        r = self.f32s[5 + self.rr("r", 2)]
        self.act(r[0:parts], ps2[0:parts], AF.Sqrt, bias=self.eps[0:parts], scale=1.0 / ndim)
        self.recip(r[0:parts], r[0:parts])
        return r

    def proj_qk_da(self, s):
        sl = slice(s * TT, (s + 1) * TT)
        kview = self.gview(self.gin[s], 0, R_KDA, 512, "(u d t) -> u d t", "kda", u=16, d=128)
        st = {}

        def s1(i):
            isq, u = i // 16, i % 16
            w = self.wbv(KC, 128)
            self.dma(w, self.wtile("win", (T_Q if isq == 0 else T_K) + u, 128))
            ps = self.ps(self.rr("pA", 4))
            for kc in range(KC):
                self.mm(ps, w[:, kc, :], self.hT[kc], kc == 0, kc == KC - 1)
            sq = self.sq16[self.rr("sq", 5)]
            self.act(sq, ps, AF.Square)
            st[i] = (ps, sq)

        def s2(i):
            isq, u = i // 16, i % 16
            ps, sq = st.pop(i)
            ps2 = self.ps(4 + self.rr("pB", 2))
            self.mm(ps2, self.ones_b, sq, True, True)
            r = self.rstd_from(ps2, 128)
            ob = self.ob[self.rr("ob", 6)]
            g = self.small[:, 0:1] if isq == 0 else self.cst[:, C_GKDA:C_GKDA + 1]
            self.stt(ob, ps, g, r, ALU.mult, ALU.mult)
            if isq == 0:
                self.dma(self.dv(self.qda_d, ("q", u, s), (u, slice(None), sl)), ob, eng="pool")
            else:
                self.dma(kview.v(kview.ap[u]), ob, eng="pool")

        self.pipe(32, s1, s2)

    def proj_v_da(self, s):
        vview = self.gview(self.gin[s], 0, R_VDA, 512, "(h t jl e) -> h t jl e", "vda", h=8, t=128, jl=4)
        for g in range(8):
            w = self.sb(self.B4 + 11264 * (g % 2), BF16, [KC, 256])
            self.dma(w, self.wtile("winv", g, 256))
            vb = self.vb[self.rr("vb", 3)]
            for tb in range(4):
                ps = self.ps(self.rr("pA", 4), 256)
                for kc in range(KC):
                    self.mm(ps, self.hT[kc][:, tb * 128:(tb + 1) * 128], w[:, kc, :], kc == 0, kc == KC - 1)
                self.act(vb[:, tb, :], ps, AF.Copy)
            self.dma(vview.v(vview.ap[g]), vb, eng="pool")

    def proj_gates(self, s):
        sl = slice(s * TT, (s + 1) * TT)
        for u in range(32):
            w = self.wbv(KC, 128)
            self.dma(w, self.wtile("win", T_GA + u, 128))
            ps = self.ps(self.rr("pA", 4))
            for kc in range(KC):
                self.mm(ps, w[:, kc, :], self.hT[kc], kc == 0, kc == KC - 1)
            ob = self.ob[self.rr("ob", 6)]
            self.act(ob, ps, AF.Sigmoid)
            self.dma(self.dv(self.gT_d, ("g", u, s), (u, slice(None), sl)), ob, eng="pool")

    def latent(self, tbase, nch, gcol, f32buf, nbuf):
        ps2 = self.ps(6 + self.rr("pC", 2))
        for i in range(nch):
            w = self.wbv(KC, 128)
            self.dma(w, self.wtile("win", tbase + i, 128))
            ps = self.ps(self.rr("pA", 4))
            for kc in range(KC):
                self.mm(ps, w[:, kc, :], self.hT[kc], kc == 0, kc == KC - 1)
            self.act(f32buf[i], ps, AF.Copy)
            sq = self.sq16[self.rr("sq", 5)]
            self.act(sq, ps, AF.Square)
            self.mm(ps2, self.ones_b, sq, i == 0, i == nch - 1)
        r = self.rstd_from(ps2, nch * 128)
        for i in range(nch):
            self.stt(nbuf[i], f32buf[i], self.cst[:, gcol + i:gcol + i + 1], r, ALU.mult, ALU.mult)

    def rope(self, xr, out_bf):
        psr = self.ps(4 + self.rr("pB", 2), 512, 64)
        self.mm(psr, self.rmat, xr, True, True)
        t1, t2 = self.f32s[7][0:64], self.f32s[8][0:64]
        self.tt(t1, xr, self.cos2, ALU.mult)
        self.tt(t2, psr, self.sin2, ALU.mult)
        self.tt(out_bf, t1, t2, ALU.add)

    def proj_mla(self, s):
        sl = slice(s * TT, (s + 1) * TT)
        U = self.B2
        cqf = [self.sb(U + 2048 * i, F32, [TT]) for i in range(6)]
        cqn = [self.sb(U + 12288 + 1024 * i, BF16, [TT]) for i in range(6)]
        ckf = [self.sb(U + 18432 + 2048 * i, F32, [TT]) for i in range(4)]
        ckn = [self.sb(U + 26624 + 1024 * i, BF16, [TT]) for i in range(4)]
        krb = self.sb(U + 30720, F32, [TT], 64)
        sqkr = self.sb(U + 32768, BF16, [TT], 64)
        xrq = self.sb(U + 34816, F32, [TT], 64)
        krg = self.sb(U + 36864, F32, [TT], 64)
        self.dma(self.cos2, Buf(self.rope_d.ap[:, 0, sl], self.rope_d.ts))
        self.dma(self.sin2, Buf(self.rope_d.ap[:, 1, sl], self.rope_d.ts))
        self.dma(self.sinS, Buf(self.rope_d.ap[:, 2, sl], self.rope_d.ts))
        self.latent(T_CQ, 6, C_GQA, cqf, cqn)
        st = {}

        cg, sg = self.f32s[7][0:64], self.f32s[8][0:64]
        self.ts(cg, self.cos2, self.small[0:64, 2:3], ALU.mult)
        self.ts(sg, self.sinS, self.small[0:64, 6:7], ALU.mult)

        def q1(h):
            w = self.wbv(6, 256)
            self.dma(w, self.wtile("wqb", h, 256))
            psn = self.ps(self.rr("pA", 4))
            psr = self.ps(self.rr("pA", 4), 512, 64)
            psw = self.ps(4 + self.rr("pB", 2), 512, 64)
            for kc in range(6):
                self.mm(psn, w[:, kc, 0:128], cqn[kc], kc == 0, kc == 5)
            for kc in range(6):
                self.mm(psr, w[:, kc, 128:192], cqn[kc], kc == 0, kc == 5)
            for kc in range(6):
                self.mm(psw, w[:, kc, 192:256], cqn[kc], kc == 0, kc == 5)
            sqn = self.sq16[self.rr("sq", 5)]
            sqr = self.sq16[self.rr("sq", 5)]
            self.act(sqn, psn, AF.Square)
            self.act(sqr[0:64], psr, AF.Square)
            st[h] = (psn, psr, psw, sqn, sqr)

        def q2(h):
            psn, psr, psw, sqn, sqr = st.pop(h)
            ps2 = self.ps(6 + self.rr("pC", 2))
            self.mm(ps2, self.ones_b, sqn, True, False)
            self.mm(ps2, self.ones_b[0:64, :], sqr[0:64], False, True)
            r = self.rstd_from(ps2, 192)
            ob = self.ob[self.rr("ob", 6)]
            self.stt(ob, psn, self.small[:, 1:2], r, ALU.mult, ALU.mult)
            self.dma(self.dv(self.qmn_d, ("q", h, s), (h, slice(None), sl)), ob, eng="pool")
            self.tt(xrq, psr, cg, ALU.mult)
            self.tt(krg, psw, sg, ALU.mult)
            self.tt(xrq, xrq, krg, ALU.add)
            ob2 = self.ob[self.rr("ob", 6)]
            self.tt(ob2[0:64], xrq, r[0:64], ALU.mult)
            self.dma(self.dv(self.qmr_d, ("q", h, s), (h, slice(None), sl)), ob2[0:64], eng="pool")

        self.pipe(16, q1, q2)
        self.latent(T_CKV, 4, C_GKVA, ckf, ckn)
        w = self.wbv(KC, 128)
        self.dma(w, self.wtile("win", T_KR, 128))
        pskr = self.ps(self.rr("pA", 4), 512, 64)
        for kc in range(KC):
            self.mm(pskr, w[:, kc, 0:64], self.hT[kc], kc == 0, kc == KC - 1)
        self.act(sqkr, pskr, AF.Square)
        self.ts(krg, pskr, self.cst[0:64, C_GKMR:C_GKMR + 1], ALU.mult)
        self.rope(krg, krb)
        knview = self.gview(self.gin[s], 0, R_KMN, 512, "(u d t) -> u d t", "kmn", u=16, d=128)
        krview = self.gview(self.gin[s], 0, R_KMR, 256, "(u d t) -> u d t", "kmr", u=16, d=64)

        def k1(h):
            w = self.wbv(4, 128)
            self.dma(w, self.wtile("wkvbk", h, 128))
            psn = self.ps(self.rr("pA", 4))
            for kc in range(4):
                self.mm(psn, w[:, kc, :], ckn[kc], kc == 0, kc == 3)
            sqn = self.sq16[self.rr("sq", 5)]
            self.act(sqn, psn, AF.Square)
            st[h] = (psn, sqn)

        def k2(h):
            psn, sqn = st.pop(h)
            ps2 = self.ps(6 + self.rr("pC", 2))
            self.mm(ps2, self.ones_b, sqn, True, False)
            self.mm(ps2, self.ones_b[0:64, :], sqkr, False, True)
            r = self.rstd_from(ps2, 192)
            ob = self.ob[self.rr("ob", 6)]
            self.stt(ob, psn, self.cst[:, C_GKMN:C_GKMN + 1], r, ALU.mult, ALU.mult)
            self.dma(knview.v(knview.ap[h]), ob, eng="pool")
            ob2 = self.ob[self.rr("ob", 6)]
            self.tt(ob2[0:64], krb, r[0:64], ALU.mult)
            self.dma(krview.v(krview.ap[h]), ob2[0:64], eng="pool")

        self.pipe(16, k1, k2)
        vview = self.gview(self.gin[s], 0, R_VM, 512, "(h t jl e) -> h t jl e", "vm", h=16, t=128, jl=4)
        for g in range(8):
            w = self.wbv(4, 256)
            self.dma(w, self.wtile("wkvbv", g, 256))
            vb = self.vb[self.rr("vb", 3)]
            for tb in range(4):
                ps = self.ps(self.rr("pA", 4), 256)
                for kc in range(4):
                    self.mm(ps, ckn[kc][:, tb * 128:(tb + 1) * 128], w[:, kc, :], kc == 0, kc == 3)
                self.act(vb[:, tb, :], ps, AF.Copy)
            for hl in range(2):
                self.dma(vview.v(vview.ap[2 * g + hl]), vb[:, :, hl * 128:(hl + 1) * 128], eng="pool")

    def phase1(self, s):
        sl = slice(s * TT, (s + 1) * TT)
        for kc in range(KC):
            self.dma(self.xres[kc], self.xT.v(self.xT.ap[kc * 128:(kc + 1) * 128, sl]))
        self.rmsnorm(C_GF1, 6 + s % 2)
        self.ffn("f1")
        for kc in range(KC):
            self.dma(self.dv(self.x1_d, ("x1", kc, s), (kc, slice(None), sl)), self.xres[kc], eng="pool")
        self.rmsnorm(C_GMIX, 6 + (s + 1) % 2)
        self.proj_qk_da(s)
        self.proj_v_da(s)
        self.proj_gates(s)
        self.proj_mla(s)
        keys = ["kda", "vda", "kmn", "kmr", "vm"]
        src = Buf(self.gin[s].ap(), [self.dT((self.gin[s].name, k)) for k in keys])
        dst = Buf(self.gout[s].ap(), [self.dT((self.gout[s].name, "all"))])
        self.allgather(dst, src)

    def load_kv(self, dst, s_pair, row0, nrows, pattern, idx, sizes, eng="sp"):
        for half in range(2):
            go = self.gout[s_pair + half]
            for r in range(8):
                v = self.gview(go, r, row0, nrows, pattern, "all", **sizes)
                src = v.v(v.ap[idx])
                p0 = 32 * half + 4 * r
                self.dma(dst[:, p0:p0 + 4, :], src, eng=eng)

    def attention(self, b):
        A = self.B0
        sb = self.sb
        vbuf = [sb(A + 32768 * i, BF16, [64, 256]) for i in range(2)]
        kbuf = [sb(A + 65536 + 16384 * i, BF16, [64, 128]) for i in range(2)]
        krbuf = [sb(A + 98304 + 16384 * i, BF16, [64, 128], 64) for i in range(2)]
        Q = A + 131072
        qn = [sb(Q + 2048 * i, BF16, [1024]) for i in range(2)]
        qr = [sb(Q + 4096 + 2048 * i, BF16, [1024], 64) for i in range(2)]
        pt = [sb(Q + 8192 + 1024 * i, BF16, [TT]) for i in range(4)]
        tmpb = [sb(Q + 34816 + 512 * i, F32, [128]) for i in range(3)]
        o1 = [[sb(Q + 13312 + 4096 * q + 2048 * e, F32, [TT]) for e in range(2)] for q in range(2)]
        rinv = sb(Q + 21504, F32, [TT])
        tf = sb(Q + 23552, F32, [TT])
        sqb = [sb(Q + 25600 + 2048 * i, BF16, [TT]) for i in range(2)]
        rsub = sb(Q + 29696, F32, [TT])
        yb = [sb(Q + 31744 + 1024 * i, BF16, [TT]) for i in range(3)]
        assert Q + 36352 <= self.arena_bytes
        sp = 2 * b
        col0 = b * 1024
        neglam = self.small[:, 5:6]

        def kloop(qh, kb, krb_, qnb, qrb, vb_, nec, bias_fn, diag_fn, banks):
            psO = [self.ps(banks[e]) for e in range(nec)]
            psS_ = self.ps(banks[2])
            nkb = 32 * (qh + 1)
            stA = {}

            def qk(m):
                g, mp = m // 8, m % 8
                jlo = max(g, 4 * qh)
                n = 128 * (4 * qh + 4 - jlo)
                c0 = 128 * (jlo - 4 * qh)
                q0 = 128 * jlo
                ps = self.ps(self.rr("pS", 3), n)
                km = kpos(m)
                if krb_ is None:
                    self.mm(ps, kb[:, km, :], qnb[:, q0:q0 + n], True, True)
                else:
                    self.mm(ps, kb[:, km, :], qnb[:, q0:q0 + n], True, False)
                    self.mm(ps, krb_[:, km, :], qrb[:, q0:q0 + n], False, True)
                p = pt[self.rr("pt", 4)]
                j = jlo
                if g >= 4 * qh:
                    tb_ = tmpb[self.rr("tmpb", 3)]
                    diag_fn(tb_, ps[:, 0:128], mp)
                    self.act(p[:, 0:128], tb_, AF.Exp)
                    j = jlo + 1
                bias_fn(p, ps, j, jlo, g, mp, 4 * qh + 4)
                stA[m] = (p, n, c0, km)

            def pv(m):
                p, n, c0, km = stA.pop(m)
                first, last = (m == 0), (m == nkb - 1)
                for e in range(nec):
                    self.mm(psO[e][:, c0:TT], vb_[:, km, e * 128:(e + 1) * 128], p[:, 0:n], first, last)
                self.mm(psS_[:, c0:TT], self.ones_b, p[:, 0:n], first, last)

            LA = 2
            for m in range(nkb + LA):
                if m < nkb:
                    qk(m)
                if m >= LA:
                    pv(m - LA)
            return psO, psS_

        for h in range(8):
            vb_ = vbuf[self.rr("vbuf", 2)]
            self.load_kv(vb_, sp, R_VDA, 512, "(h t jl e) -> h t jl e", h, dict(h=8, t=128, jl=4))
            for mp_ in range(2):
                u = 2 * h + mp_
                kb = kbuf[self.rr("kbuf", 2)]
                self.load_kv(kb, sp, R_KDA, 512, "(u d t) -> u d t", u, dict(u=16, d=128))
                qnb = qn[self.rr("qn", 2)]
                self.dma(qnb, Buf(self.qda_d[u, :, col0:col0 + 1024], [self.dT((self.qda_d.name, ("q", u, 2 * b))),
                                                                       self.dT((self.qda_d.name, ("q", u, 2 * b + 1)))]))
                slope = SLOPES[h]

                def bias_fn(p, ps, j, jlo, g, mp, jend, h=h):
                    while j < jend:
                        cs = slice(128 * (j - jlo), 128 * (j - jlo + 1))
                        bcol = (j - g) * 8 + mp
                        self.act(p[:, cs], ps[:, cs], AF.Exp, bias=self.btab[:, h, bcol:bcol + 1])
                        j += 1

                def diag_fn(tb_, pscols, mp, slope=slope):
                    self.stt(tb_, self.ac[:, mp * 128:(mp + 1) * 128], slope, pscols, ALU.mult, ALU.add)

                for qh in range(2):
                    par = self.rr("oset", 2)
                    banks = (3, 4, 5) if par == 0 else (6, 7, 5)
                    psO, psS_ = kloop(qh, kb, None, qnb, None, vb_, 2, bias_fn, diag_fn, banks)
                    self.recip(rinv, psS_)
                    if mp_ == 0:
                        for e in range(2):
                            self.tt(o1[qh][e], psO[e], rinv, ALU.mult)
                    else:
                        for e in range(2):
                            self.tt(tf, psO[e], rinv, ALU.mult)
                            self.stt(o1[qh][e], tf, neglam, o1[qh][e], ALU.mult, ALU.add)
                        ps2 = self.ps(self.rr("pS", 3))
                        for e in range(2):
                            self.act(sqb[e], o1[qh][e], AF.Square)
                            self.mm(ps2, self.ones_b, sqb[e], e == 0, e == 1)
                        self.act(rsub, ps2, AF.Sqrt, bias=self.eps, scale=1.0 / 256)
                        self.recip(rsub, rsub)
                        for e in range(2):
                            y = yb[self.rr("yb", 3)]
                            self.stt(y, o1[qh][e], self.small[:, 3 + e:4 + e], rsub, ALU.mult, ALU.mult)
                            c = col0 + qh * TT
                            self.dma(self.dv(self.yT_d, ("y", 2 * h + e, 2 * b + qh), (2 * h + e, slice(None), slice(c, c + TT))), y, eng="pool")
        for h in range(16):
            vb_ = vbuf[self.rr("vbuf", 2)]
            vbm = vb_.v(vb_.ap[:, :, 0:128])
            self.load_kv(vbm, sp, R_VM, 512, "(h t jl e) -> h t jl e", h, dict(h=16, t=128, jl=4))
            kb = kbuf[self.rr("kbuf", 2)]
            self.load_kv(kb, sp, R_KMN, 512, "(u d t) -> u d t", h, dict(u=16, d=128))
            krb_ = krbuf[self.rr("krbuf", 2)]
            self.load_kv(krb_, sp, R_KMR, 256, "(u d t) -> u d t", h, dict(u=16, d=64))
            qnb = qn[self.rr("qn", 2)]
            qrb = qr[self.rr("qr", 2)]
            ts_ = [self.dT((self.qmn_d.name, ("q", h, 2 * b))), self.dT((self.qmn_d.name, ("q", h, 2 * b + 1)))]
            self.dma(qnb, Buf(self.qmn_d[h, :, col0:col0 + 1024], ts_))
            ts_ = [self.dT((self.qmr_d.name, ("q", h, 2 * b))), self.dT((self.qmr_d.name, ("q", h, 2 * b + 1)))]
            self.dma(qrb, Buf(self.qmr_d[h, :, col0:col0 + 1024], ts_))

            def bias_fn(p, ps, j, jlo, g, mp, jend):
                if j < jend:
                    cs = slice(128 * (j - jlo), 128 * (jend - jlo))
                    self.act(p[:, cs], ps[:, cs], AF.Exp)

            def diag_fn(tb_, pscols, mp):
                self.tt(tb_, pscols, self.mmk[:, mp * 128:(mp + 1) * 128], ALU.add)

            for qh in range(2):
                par = self.rr("oset", 2)
                banks = (3, 4, 5) if par == 0 else (6, 7, 5)
                psO, psS_ = kloop(qh, kb, krb_, qnb, qrb, vbm, 1, bias_fn, diag_fn, banks)
                self.recip(rinv, psS_)
                y = yb[self.rr("yb", 3)]
                self.tt(y, psO[0], rinv, ALU.mult)
                c = col0 + qh * TT
                self.dma(self.dv(self.yT_d, ("y", 16 + h, 2 * b + qh), (16 + h, slice(None), slice(c, c + TT))), y, eng="pool")

    def phase3(self, s):
        sl = slice(s * TT, (s + 1) * TT)
        U = self.B2
        ya = [self.sb(U + 1024 * i, BF16, [TT]) for i in range(16)]
        yb_ = [self.sb(U + 16384 + 1024 * i, BF16, [TT]) for i in range(16)]
        for i in range(16):
            self.dma(ya[i], self.dv(self.yT_d, ("y", i, s), (i, slice(None), sl)))
            self.dma(yb_[i], self.dv(self.yT_d, ("y", 16 + i, s), (16 + i, slice(None), sl)))
        for oc in range(16):
            wa_, wb_ = self.wbv(KC, 128), self.wbv(KC, 128)
            self.dma(wa_, self.wtile("wa", oc, 128))
            self.dma(wb_, self.wtile("wb", oc, 128))
            ga, gb = self.ob[self.rr("ob", 6)], self.ob[self.rr("ob", 6)]
            self.dma(ga, self.dv(self.gT_d, ("g", oc, s), (oc, slice(None), sl)))
            self.dma(gb, self.dv(self.gT_d, ("g", 16 + oc, s), (16 + oc, slice(None), sl)))
            pa, pb = self.ps(self.rr("pA", 4)), self.ps(self.rr("pA", 4))
            for kc in range(KC):
                self.mm(pa, wa_[:, kc, :], ya[kc], kc == 0, kc == KC - 1)
            for kc in range(KC):
                self.mm(pb, wb_[:, kc, :], yb_[kc], kc == 0, kc == KC - 1)
            m1 = self.f32s[self.rr("sq", 5)]
            self.tt(m1, pa, ga, ALU.mult)
            m2 = self.f32s[self.rr("sq", 5)]
            self.tt(m2, pb, gb, ALU.mult)
            self.tt(self.hT[oc], m1, m2, ALU.add)
        for kc in range(KC):
            self.dma(self.xres[kc], self.dv(self.x1_d, ("x1", kc, s), (kc, slice(None), sl)))
        for oc in range(16):
            w = self.wbv(KC, 128)
            self.dma(w, self.wtile("wo", oc, 128))
            ps = self.ps(4 + self.rr("pB", 2))
            for kc in range(KC):
                self.mm(ps, w[:, kc, :], self.hT[kc], kc == 0, kc == KC - 1)
            self.tt(self.xres[oc], ps, self.xres[oc], ALU.add)
        self.rmsnorm(C_GF2, 6 + s % 2)
        self.ffn("f2")
        for kc in range(KC):
            self.dma(self.out_d.v(self.out_d.ap[kc * 128:(kc + 1) * 128, sl]), self.xres[kc], eng="pool")

    def build(self):
        self.setup()
        self.consts()
        first = ["f1w1", "f1w3", "f1w2", "win", "winv", "wqb", "wkvbk", "wkvbv"]
        rest = ["wa", "wb", "wo", "f2w1", "f2w3", "f2w2"]
        for n in first:
            self.convert([n])
            self.gather_w([n])
        self.convert(rest)
        for s in range(NSLOT):
            self.phase1(s)
            if s == 0:
                self.gather_w(rest)
        for b in range(2):
            self.attention(b)
        for s in range(NSLOT):
            self.phase3(s)
        if self.stage == 50:
            for hnd, dt in [(self.x1_d, F32), (self.qda_d, BF16), (self.qmn_d, BF16), (self.qmr_d, BF16), (self.gT_d, BF16),
                            (self.yT_d, BF16), (self.gin[0], BF16), (self.gin[3], BF16), (self.gout[1], BF16)]:
                o = self.dram_out("dbg_" + hnd.name, list(hnd.shape), dt)
                ts_ = [t for k, t in self.dram.items() if isinstance(k, tuple) and k[0] == hnd.name]
                self.dma(o, Buf(hnd.ap(), ts_))
        return self.finish()

    def finish(self):
        nc = self.nc
        sems = self.p.finalize()
        semh = {}
        for i, s in enumerate(sems):
            semh[s] = self.es.enter_context(nc.semaphore("s%d" % i))
        finals = {}
        for e in ENGS:
            for ins in self.p.streams[e]:
                if ins.kind != "c":
                    finals[ins.sem] = max(finals.get(ins.sem, 0), ins.val)
        fw = sorted(finals.items())
        p = self.p
        with nc.Block() as block:
            @block.tensor
            def _(e):
                p.replay("pe", e, semh)

            @block.scalar
            def _(e):
                p.replay("act", e, semh)

            @block.vector
            def _(e):
                p.replay("dve", e, semh)

            @block.gpsimd
            def _(e):
                p.replay("pool", e, semh)

            @block.sync
            def _(e):
                p.replay("sp", e, semh, final_waits=fw)
        self.es.close()
        return nc


def _tok_index(c):
    idx = np.empty((8, 128), np.int64)
    for j in range(8):
        idx[j, :] = (8 * j + c) * 128 + np.arange(128)
    return idx.reshape(1024)


def _consts(c, inp):
    cst = np.zeros((128, CST_W), np.float32)
    cst[:, C_ONES:C_ONES + 128] = 1.0
    for i in range(32):
        cst[i + 32, C_RMAT + i] = -1.0
        cst[i, C_RMAT + 32 + i] = 1.0

    def fm(v, n):
        return np.asarray(v, np.float32).reshape(n, 128).T
    cst[:, C_GF1:C_GF1 + 16] = fm(inp["ffn1_norm_g"][0], 16)
    cst[:, C_GMIX:C_GMIX + 16] = fm(inp["mix_norm_g"][0], 16)
    cst[:, C_GF2:C_GF2 + 16] = fm(inp["ffn2_norm_g"][0], 16)
    cst[:, C_GQA:C_GQA + 6] = fm(inp["mla_q_a_norm_g"][0], 6)
    cst[:, C_GKVA:C_GKVA + 4] = fm(inp["mla_kv_a_norm_g"][0], 4)
    cst[:, C_GQDA] = inp["da_q_norm_g"][0]
    cst[:, C_GKDA] = inp["da_k_norm_g"][0]
    cst[:, C_GQMN] = inp["mla_q_norm_g"][0][:128]
    cst[:64, C_GQMR] = inp["mla_q_norm_g"][0][128:]
    cst[:32, C_GQMRW] = inp["mla_q_norm_g"][0][160:]
    cst[32:64, C_GQMRW] = inp["mla_q_norm_g"][0][128:160]
    cst[:, C_GKMN] = inp["mla_k_norm_g"][0][:128]
    cst[:64, C_GKMR] = inp["mla_k_norm_g"][0][128:]
    cst[:, C_SUB0] = inp["da_subln_g"][0][:128]
    cst[:, C_SUB1] = inp["da_subln_g"][0][128:]
    cst[:, C_EPS] = EPS
    for i, n in enumerate(["da_lambda_q1", "da_lambda_k1", "da_lambda_q2", "da_lambda_k2"]):
        cst[:, C_LAM + i] = inp[n][0]
    p = np.arange(128, dtype=np.float64)
    for dj in range(8):
        for m in range(8):
            dist = 8 * dj + c - m
            cst[:, C_A2 + dj * 8 + m] = (p - 64 - 128 * dist) if dist >= 1 else NEG
    f = np.arange(128, dtype=np.float64)
    chunk_ok = (p[:, None] // 64) <= (f[None, :] // 64)
    for m in range(8):
        if m < c:
            a = np.broadcast_to((p - 64 + 128 * (m - c))[:, None], (128, 128))
            mk = np.zeros((128, 128))
        elif m == c:
            a = np.where(chunk_ok, -np.abs(f[None, :] - p[:, None]) + (f[None, :] - 64), NEG)
            mk = np.where(chunk_ok, 0.0, -30000.0)
        else:
            a = np.full((128, 128), NEG)
            mk = np.full((128, 128), -30000.0)
        cst[:, C_AC + m * 128:C_AC + (m + 1) * 128] = a
        cst[:, C_MM + m * 128:C_MM + (m + 1) * 128] = mk
    return cst


def _rope_tab(c):
    pos = np.concatenate([_tok_index(c), _tok_index(c)]).astype(np.float32)
    inv = (10000.0 ** (-np.arange(0, 64, 2, dtype=np.float32) / 64)).astype(np.float32)
    ang = pos[None, :] * inv[:, None]
    cs, sn = np.cos(ang).astype(np.float32), np.sin(ang).astype(np.float32)
    tab = np.empty((64, 3, NTOK), np.float32)
    tab[:32, 0], tab[32:, 0] = cs, cs
    tab[:32, 1], tab[32:, 1] = sn, sn
    tab[:32, 2], tab[32:, 2] = -sn, sn
    return tab


def _fm_tiles(W, c0, ntile, M, mt, stride=None):
    K = W.shape[0]
    kcn = K // 128
    stride = M if stride is None else stride
    out = np.zeros((ntile, 128, kcn, mt), np.float32)
    for i in range(ntile):
        blk = W[:, c0 + i * stride:c0 + i * stride + M].reshape(kcn, 128, M)
        out[i, :, :, :M] = blk.transpose(1, 0, 2)
    return out.reshape(ntile, 128, kcn * mt)


def _weight_groups(inp):
    g = lambda n: np.asarray(inp[n], np.float32)[0]
    W = {}
    for k, n in (("f1", "ffn1"), ("f2", "ffn2")):
        W[k + "w1"] = _fm_tiles(g(n + "_w1"), 0, 44, 128, 128)
        W[k + "w3"] = _fm_tiles(g(n + "_w3"), 0, 44, 128, 128)
        W[k + "w2"] = _fm_tiles(g(n + "_w2"), 0, 16, 128, 128)
    win = g("w_in")
    parts = [_fm_tiles(win, 0, 16, 128, 128), _fm_tiles(win, 2048, 16, 128, 128), _fm_tiles(win, 6144, 6, 128, 128),
             _fm_tiles(win, 6912, 4, 128, 128), _fm_tiles(win, 7424, 1, 64, 128), _fm_tiles(win, 7488, 32, 128, 128)]
    W["win"] = np.concatenate(parts, 0)
    W["winv"] = _fm_tiles(win, 4096, 8, 256, 256)
    wq = g("mla_w_qb").reshape(768, 16, 192)
    wq = np.concatenate([wq, wq[:, :, 160:192], wq[:, :, 128:160]], axis=2).reshape(768, 16 * 256)
    W["wqb"] = _fm_tiles(wq, 0, 16, 256, 256)
    kvb = g("mla_w_kvb")
    W["wkvbk"] = _fm_tiles(kvb, 0, 16, 128, 128, stride=256)
    v = np.zeros((8, 128, 4, 256), np.float32)
    for gi in range(8):
        for hl in range(2):
            h = 2 * gi + hl
            v[gi, :, :, hl * 128:(hl + 1) * 128] = kvb[:, h * 256 + 128:h * 256 + 256].reshape(4, 128, 128).transpose(1, 0, 2)
    W["wkvbv"] = v.reshape(8, 128, 1024)
    W["wa"] = _fm_tiles(g("w_branch_a"), 0, 16, 128, 128)
    W["wb"] = _fm_tiles(g("w_branch_b"), 0, 16, 128, 128)
    W["wo"] = _fm_tiles(g("w_out"), 0, 16, 128, 128)
    out = {}
    for n, nt, e in WG:
        a = W[n]
        if a.shape[0] < nt:
            a = np.concatenate([a, np.zeros((nt - a.shape[0], 128, e), np.float32)], 0)
        assert a.shape == (nt, 128, e), (n, a.shape)
        out[n] = a.reshape(8, nt // 8 * 128, e)
    return out


def make_in_maps(inp):
    x = np.asarray(inp["x"], np.float32)
    wgs = _weight_groups(inp)
    maps = []
    for c in range(NCORES):
        ti = _tok_index(c)
        xl = np.concatenate([x[0, ti], x[1, ti]], axis=0)
        m = {"xT": np.ascontiguousarray(xl.T), "cst": _consts(c, inp), "rope": _rope_tab(c)}
        for n, _, _ in WG:
            m["ws_" + n] = np.ascontiguousarray(wgs[n][c])
        maps.append(m)
    return maps


def assemble(results):
    out = np.empty((2, SEQ, D), np.float32)
    for c in range(NCORES):
        ti = _tok_index(c)
        o = np.asarray(results[c]["outT"]).T
        out[0, ti] = o[:1024]
        out[1, ti] = o[1024:]
    return out


def kernel(**inputs):
    nc = Kern(99).build()
    res = run_bass_kernel_spmd(nc, make_in_maps(inputs), core_ids=list(range(NCORES)))
    return assemble(res.results)
```

```python
import math
from contextlib import ExitStack

import numpy as np
import ml_dtypes
import concourse.bass as bass
import concourse.mybir as mybir
from concourse.bass_utils import run_bass_kernel_spmd

F32 = mybir.dt.float32
BF16 = mybir.dt.bfloat16
AF = mybir.ActivationFunctionType
ALU = mybir.AluOpType

NCORES = 8
D = 2048
DFF = 5632
NF = DFF // 128
KC = D // 128
SEQ = 8192
NTOK = 2048
TT = 512
NSLOT = 4
EPS = 1e-6
LAMBDA_INIT = 0.8 - 0.6 * math.exp(-0.3 * 0)
NEG = -1.0e6

ENGS = ("pe", "act", "dve", "pool", "sp")
NDMASEM = {"sp": 40, "pool": 24, "act": 8}


class T:
    __slots__ = ("w", "r", "rd")

    def __init__(self):
        self.w = None
        self.r = {}
        self.rd = []


class Buf:
    __slots__ = ("ap", "ts")

    def __init__(self, ap, ts):
        self.ap = ap
        self.ts = ts

    def __getitem__(self, idx):
        return Buf(self.ap[idx], self.ts)

    def v(self, ap):
        return Buf(ap, self.ts)


class Ins:
    __slots__ = ("eng", "fn", "waits", "signal", "sem", "val", "kind", "prev")

    def __init__(self, eng, fn, kind):
        self.eng = eng
        self.fn = fn
        self.kind = kind
        self.waits = []
        self.signal = False
        self.sem = None
        self.val = 0
        self.prev = None


class Prog:
    def __init__(self):
        self.streams = {e: [] for e in ENGS}

    def emit(self, eng, fn, reads=(), writes=(), kind="c"):
        ins = Ins(eng, fn, kind)
        deps = {}
        true_deps = set()
        for b in reads:
            for t in b.ts:
                if t.w is not None:
                    deps[id(t.w)] = t.w
                    true_deps.add(id(t.w))
        for b in writes:
            for t in b.ts:
                if t.w is not None:
                    deps[id(t.w)] = t.w
                    true_deps.add(id(t.w))
                for r in t.r.values():
                    deps[id(r)] = r
                for r in t.rd:
                    deps[id(r)] = r
        for d in deps.values():
            if d.kind == "c" and d.eng == eng and kind == "c":
                if eng == "pe" or id(d) not in true_deps:
                    continue
            d.signal = True
            ins.waits.append(d)
        for b in reads:
            for t in b.ts:
                if kind == "c":
                    t.r[eng] = ins
                else:
                    t.rd.append(ins)
        for b in writes:
            for t in b.ts:
                t.w = ins
                t.r = {}
                t.rd = []
        self.streams[eng].append(ins)
        return ins

    def finalize(self):
        sems = set()
        for e in ENGS:
            cnt, semi, k, ncc = 0, 0, 0, 0
            P = NDMASEM.get(e, 1)
            for ins in self.streams[e]:
                if ins.kind == "c":
                    if ins.signal:
                        cnt += 1
                        if cnt > 30000:
                            semi += 1
                            cnt = 1
                        ins.sem = ("c", e, semi)
                        ins.val = cnt
                        sems.add(ins.sem)
                elif ins.kind == "d":
                    ins.sem = ("d", e, k % P)
                    ins.val = 16 * (k // P + 1)
                    if k >= P:
                        ins.prev = (ins.sem, 16 * (k // P))
                    k += 1
                    sems.add(ins.sem)
                else:
                    ins.sem = ("cc", e, ncc)
                    ins.val = 1
                    ncc += 1
                    sems.add(ins.sem)
        return sorted(sems)

    def replay(self, eng, e, semh, final_waits=None):
        waited = {}

        def w(sem, val):
            if waited.get(sem, 0) < val:
                e.wait_ge(semh[sem], val)
                waited[sem] = val

        for ins in self.streams[eng]:
            for d in ins.waits:
                w(d.sem, d.val)
            if ins.prev is not None:
                w(*ins.prev)
            r = ins.fn(e)
            if ins.kind == "d":
                r.then_inc(semh[ins.sem], 16)
            elif ins.kind == "cc":
                r.then_inc(semh[ins.sem], 1)
            elif ins.signal:
                r.then_inc(semh[ins.sem], 1)
        if final_waits:
            for sem, val in final_waits:
                w(sem, val)


class Builder:
    def __init__(self, stage):
        self.stage = stage
        self.nc = bass.Bass("TRN2", target_bir_lowering=False)
        self.p = Prog()
        self.es = ExitStack()
        self.sb_off = 0
        self.dram = {}

    def dram_in(self, name, shape, dt=F32):
        h = self.nc.dram_tensor(name, list(shape), dt, kind="ExternalInput")
        b = Buf(h.ap(), [T()])
        self.dram[name] = b
        return b

    def dram_out(self, name, shape, dt=F32):
        h = self.nc.dram_tensor(name, list(shape), dt, kind="ExternalOutput")
        b = Buf(h.ap(), [T()])
        self.dram[name] = b
        return b

    def dram_tmp(self, name, shape, dt=BF16):
        h = self.nc.dram_tensor(name, list(shape), dt)
        return h

    def sbuf_init(self, nbytes):
        self.arena = self.es.enter_context(self.nc.sbuf_tensor("arena", [128, nbytes // 4], F32))
        self.arena_bytes = nbytes
        self.pages = [T() for _ in range(nbytes // 512)]
        self.psum = self.es.enter_context(self.nc.psum_tensor("psum", [128, 8 * 512], F32))
        self.pspages = [T() for _ in range(8 * 4)]

    def sb(self, off, dt, free_shape, parts=128):
        esz = 4 if dt == F32 else 2
        n = int(np.prod(free_shape))
        assert off % 4 == 0 and off + n * esz <= self.arena_bytes, (off, n, esz)
        base = self.arena.bitcast(dt) if dt != F32 else self.arena
        ap = base[0:parts, off // esz: off // esz + n]
        if len(free_shape) == 2:
            ap = ap.rearrange("p (a b) -> p a b", a=free_shape[0])
        elif len(free_shape) == 3:
            ap = ap.rearrange("p (a b c) -> p a b c", a=free_shape[0], b=free_shape[1])
        ts = self.pages[off // 512: (off + n * esz + 511) // 512]
        return Buf(ap, ts)

    def ps(self, bank, cols=512, parts=128, coff=0):
        ap = self.psum[0:parts, bank * 512 + coff: bank * 512 + coff + cols]
        ts = self.pspages[bank * 4 + coff // 128: bank * 4 + (coff + cols + 127) // 128]
        return Buf(ap, ts)

    def dma(self, out, in_, eng="sp"):
        o, i = out.ap, in_.ap
        return self.p.emit(eng, lambda e: e.dma_start(out=o, in_=i), reads=[in_], writes=[out], kind="d")

    def mm(self, out, lhsT, rhs, start, stop):
        o, l, r = out.ap, lhsT.ap, rhs.ap
        return self.p.emit("pe", lambda e: e.matmul(o, l, r, start=start, stop=stop, skip_group_check=True),
                           reads=[lhsT, rhs], writes=[out])

    def act(self, out, in_, func, bias=None, scale=None, eng="act"):
        o, i = out.ap, in_.ap
        kw = {}
        rd = [in_]
        if bias is not None:
            if isinstance(bias, Buf):
                kw["bias"] = bias.ap
                rd.append(bias)
            else:
                kw["bias"] = bias
        if scale is not None:
            if isinstance(scale, Buf):
                kw["scale"] = scale.ap
                rd.append(scale)
            else:
                kw["scale"] = scale
        return self.p.emit(eng, lambda e: e.activation(out=o, in_=i, func=func, **kw), reads=rd, writes=[out])

    def tt(self, out, a, b, op, eng="dve"):
        o, x, y = out.ap, a.ap, b.ap
        return self.p.emit(eng, lambda e: e.tensor_tensor(out=o, in0=x, in1=y, op=op), reads=[a, b], writes=[out])

    def stt(self, out, in0, scalar, in1, op0, op1, eng="dve"):
        o, x, y = out.ap, in0.ap, in1.ap
        rd = [in0, in1]
        if isinstance(scalar, Buf):
            s = scalar.ap
            rd.append(scalar)
        else:
            s = scalar
        return self.p.emit(eng, lambda e: e.scalar_tensor_tensor(out=o, in0=x, scalar=s, in1=y, op0=op0, op1=op1),
                           reads=rd, writes=[out])

    def ts(self, out, in0, s1, op0, s2=None, op1=None, eng="dve"):
        o, x = out.ap, in0.ap
        rd = [in0]
        a1 = s1
        if isinstance(s1, Buf):
            a1 = s1.ap
            rd.append(s1)
        a2 = s2
        if isinstance(s2, Buf):
            a2 = s2.ap
            rd.append(s2)
        if op1 is None:
            return self.p.emit(eng, lambda e: e.tensor_scalar(out=o, in0=x, scalar1=a1, scalar2=None, op0=op0),
                               reads=rd, writes=[out])
        return self.p.emit(eng, lambda e: e.tensor_scalar(out=o, in0=x, scalar1=a1, scalar2=a2, op0=op0, op1=op1),
                           reads=rd, writes=[out])

    def recip(self, out, in_, eng="dve"):
        o, i = out.ap, in_.ap
        return self.p.emit(eng, lambda e: e.reciprocal(out=o, in_=i), reads=[in_], writes=[out])

    def copy(self, out, in_, eng="dve"):
        o, i = out.ap, in_.ap
        return self.p.emit(eng, lambda e: e.tensor_copy(out=o, in_=i), reads=[in_], writes=[out])

    def memset(self, out, val, eng="dve"):
        o = out.ap
        return self.p.emit(eng, lambda e: e.memset(o, val), reads=[], writes=[out])


C_ONES, C_RMAT, C_GF1, C_GMIX, C_GF2, C_GQA, C_GKVA = 0, 128, 192, 208, 224, 240, 246
C_GQDA, C_GKDA, C_GQMN, C_GQMR, C_GKMN, C_GKMR, C_SUB0, C_SUB1, C_EPS, C_LAM = 250, 251, 252, 253, 254, 255, 256, 257, 258, 259
C_A2, C_AC, C_MM, CST_W = 264, 328, 1352, 2376

WIN_FM = [("q", 0, 16, 128), ("k", 2048, 16, 128), ("cq", 6144, 6, 128), ("ckv", 6912, 4, 128),
          ("kr", 7424, 1, 64), ("ga", 7488, 16, 128), ("gb", 9536, 16, 128)]
WIN_V0 = 4096


C_ONES, C_RMAT, C_GF1, C_GMIX, C_GF2, C_GQA, C_GKVA = 0, 128, 192, 208, 224, 240, 246
C_GQDA, C_GKDA, C_GQMN, C_GQMR, C_GKMN, C_GKMR, C_SUB0, C_SUB1, C_EPS, C_LAM = 250, 251, 252, 253, 254, 255, 256, 257, 258, 259
C_A2, C_AC, C_MM, CST_W = 264, 328, 1352, 2376
C_GQMRW = 263

WG = [("f1w1", 48, 2048), ("f1w3", 48, 2048), ("f1w2", 16, 5632), ("win", 80, 2048), ("winv", 8, 4096),
      ("wqb", 16, 1536), ("wkvbk", 16, 512), ("wkvbv", 8, 1024), ("wa", 16, 2048), ("wb", 16, 2048),
      ("wo", 16, 2048), ("f2w1", 48, 2048), ("f2w3", 48, 2048), ("f2w2", 16, 5632)]
WGD = {n: (t, e) for n, t, e in WG}
T_Q, T_K, T_CQ, T_CKV, T_KR, T_GA, T_GB = 0, 16, 32, 38, 42, 43, 59
GROWS = 2304
R_KDA, R_VDA, R_KMN, R_KMR, R_VM = 0, 512, 1024, 1536, 1792
SLOPES = [2.0 ** (-8.0 * (h + 1) / 8) for h in range(8)]
GKEEP = [1, 1, 1, 1, 2, 3, 6, 7]


def kpos(m):
    return 32 * (m // 32) + 4 * (m % 8) + (m % 32) // 8


class Kern(Builder):
    def dT(self, key):
        if key not in self.dram:
            self.dram[key] = T()
        return self.dram[key]

    def wtile(self, name, t, m):
        h = self.wg[name]
        ap = h[t * 128:(t + 1) * 128, :].rearrange("p (a b) -> p a b", b=m)
        return Buf(ap, [self.dT(("wg", name))])

    def dv(self, h, key, idx):
        return Buf(h[idx], [self.dT((h.name, key))])

    def gview(self, hnd, rank, row0, nrows, pattern, key, **sizes):
        tot = int(np.prod(hnd.shape))
        flat = hnd.reshape([tot])
        a = (rank * GROWS + row0) * 2048
        ap = flat[a:a + nrows * 2048].rearrange(pattern, **sizes)
        return Buf(ap, [self.dT((hnd.name, key))])

    def setup(self):
        nc = self.nc
        self.sbuf_init(204800)
        I = self.dram_in
        self.xT = I("xT", [D, NTOK])
        self.cst_d = I("cst", [128, CST_W])
        self.rope_d = I("rope", [64, 3, NTOK])
        tmp = self.dram_tmp
        self.wsrc, self.wown, self.wg = {}, {}, {}
        for n, nt, e in WG:
            self.wsrc[n] = I("ws_" + n, [nt // 8 * 128, e])
            self.wown[n] = tmp("wo_" + n, [nt // 8 * 128, e])
            self.wg[n] = tmp("wg_" + n, [nt * 128, e])
        self.out_d = self.dram_out("outT", [D, NTOK])
        self.x1_d = tmp("x1T", [KC, 128, NTOK], F32)
        self.qda_d = tmp("qda", [16, 128, NTOK])
        self.qmn_d = tmp("qmn", [16, 128, NTOK])
        self.qmr_d = tmp("qmr", [16, 64, NTOK])
        self.gT_d = tmp("gT", [32, 128, NTOK])
        self.yT_d = tmp("yT", [32, 128, NTOK])
        self.gin = [tmp("gin%d" % s, [GROWS, 2048]) for s in range(NSLOT)]
        self.gout = [tmp("gout%d" % s, [8 * GROWS, 2048]) for s in range(NSLOT)]
        sb = self.sb
        self.cst = sb(0, F32, [CST_W])
        self.ones_f = self.cst[:, C_ONES:C_ONES + 128]
        self.rmat = self.cst[0:64, C_RMAT:C_RMAT + 64]
        self.eps = self.cst[:, C_EPS:C_EPS + 1]
        self.mmk = self.cst[:, C_MM:C_MM + 1024]
        self.ones_b = sb(9728, BF16, [128])
        self.small = sb(10240, F32, [64])
        self.bpp = sb(10496, F32, [8, 8])
        self.acp = sb(C_AC * 4, BF16, [8, 128])
        self.strip = sb(C_AC * 4 + 2048, BF16, [1024])
        self.zeros_b = sb(10752, BF16, [TT])
        self.cos2 = sb(12544, F32, [TT], 64)
        self.sin2 = sb(14592, F32, [TT], 64)
        self.sinS = sb(182784, F32, [TT], 64)
        B0 = 16896
        self.B0 = B0
        self.xres = [sb(B0 + 2048 * k, F32, [TT]) for k in range(KC)]
        B1 = B0 + 32768
        self.hT = [sb(B1 + 1024 * k, BF16, [TT]) for k in range(KC)]
        B2 = B1 + 16384
        self.B2 = B2
        self.uT = [sb(B2 + 1024 * k, BF16, [TT]) for k in range(NF)]
        B3 = B2 + 45056
        self.B3 = B3
        self.wb = [sb(B3 + 4096 * k, BF16, [KC, 128]) for k in range(6)]
        B4 = B3 + 24576
        self.B4 = B4
        self.w2b = [sb(B4 + 11264 * k, BF16, [NF, 128]) for k in range(2)]
        B5 = B4 + 22528
        self.f32s = [sb(B5 + 2048 * k, F32, [TT]) for k in range(9)]
        self.sq = self.f32s[0:2]
        self.sq16 = [sb(B5 + 2048 * k, BF16, [TT]) for k in range(5)]
        self.rstd = self.f32s[2]
        self.sil = self.f32s[3:5]
        self.ob = [sb(B5 + 18432 + 1024 * k, BF16, [TT]) for k in range(6)]
        self.vb = [sb(B5 + 18432 + 2048 * k, BF16, [4, 256]) for k in range(3)]
        assert B5 + 24576 <= self.arena_bytes, B5 + 24576
        self.cnt = {}

    def rr(self, name, n):
        v = self.cnt.get(name, 0)
        self.cnt[name] = v + 1
        return v % n

    def wbv(self, kc, m):
        i = self.rr("wb", 6)
        return self.sb(self.B3 + 4096 * i, BF16, [kc, m])

    def consts(self):
        self.dma(self.cst, self.cst_d)
        self.memset(self.ones_b, 1.0)
        self.memset(self.zeros_b, 0.0)
        sm = self.small
        c = self.cst
        self.ts(sm[:, 0:1], c[:, C_GQDA:C_GQDA + 1], 128.0 ** -0.5, ALU.mult)
        self.ts(sm[:, 1:2], c[:, C_GQMN:C_GQMN + 1], 192.0 ** -0.5, ALU.mult)
        self.ts(sm[:, 2:3], c[:, C_GQMR:C_GQMR + 1], 192.0 ** -0.5, ALU.mult)
        self.ts(sm[:, 6:7], c[:, C_GQMRW:C_GQMRW + 1], 192.0 ** -0.5, ALU.mult)
        self.ts(sm[:, 3:4], c[:, C_SUB0:C_SUB0 + 1], 1.0 - LAMBDA_INIT, ALU.mult)
        self.ts(sm[:, 4:5], c[:, C_SUB1:C_SUB1 + 1], 1.0 - LAMBDA_INIT, ALU.mult)
        self.tt(sm[:, 8:9], c[:, C_LAM:C_LAM + 1], c[:, C_LAM + 1:C_LAM + 2], ALU.mult)
        self.tt(sm[:, 9:10], c[:, C_LAM + 2:C_LAM + 3], c[:, C_LAM + 3:C_LAM + 4], ALU.mult)
        ps = self.ps(7, 2)
        self.mm(ps, self.ones_f, sm[:, 8:10], True, True)
        self.act(sm[:, 10:12], ps, AF.Exp)
        self.tt(sm[:, 12:13], sm[:, 11:12], sm[:, 10:11], ALU.subtract)
        self.ts(sm[:, 5:6], sm[:, 12:13], -LAMBDA_INIT, ALU.add)
        for h in range(8):
            self.ts(self.bpp[:, h, :], c[:, C_A2:C_A2 + 8], SLOPES[h], ALU.mult)

    def convert(self, names):
        for n in names:
            nt, e = WGD[n]
            for t in range(nt // 8):
                sl = slice(t * 128, (t + 1) * 128)
                src = Buf(self.wsrc[n].ap[sl, :], self.wsrc[n].ts)
                dst = Buf(self.wown[n][sl, :], [self.dT(("wown", n))])
                self.dma(dst, src, eng="pool")

    def gather_w(self, names, qos="P3"):
        for n in names:
            src = Buf(self.wown[n].ap(), [self.dT(("wown", n))])
            dst = Buf(self.wg[n].ap(), [self.dT(("wg", n))])
            self.allgather(dst, src, None if n in ("f1w1", "f1w3") else qos)

    def allgather(self, dst, src, qos="P3"):
        o, i = dst.ap.opt(), src.ap.opt()
        self.p.emit("pool", lambda e: e.collective_compute("AllGather", ALU.bypass, replica_groups=[list(range(NCORES))],
                                                           ins=[i], outs=[o], dma_qos=qos),
                    reads=[src], writes=[dst], kind="cc")

    def rmsnorm(self, gcol, bank):
        ps = self.ps(bank)
        for kc in range(KC):
            sq = self.sq16[kc % 2]
            self.act(sq, self.xres[kc], AF.Square)
            self.mm(ps, self.ones_b, sq, kc == 0, kc == KC - 1)
        self.act(self.rstd, ps, AF.Sqrt, bias=self.eps, scale=1.0 / D)
        self.recip(self.rstd, self.rstd)
        for kc in range(KC):
            self.stt(self.hT[kc], self.xres[kc], self.cst[:, gcol + kc:gcol + kc + 1], self.rstd, ALU.mult, ALU.mult)

    def ffn(self, k):
        for f in range(NF):
            b1, b3 = self.wbv(KC, 128), self.wbv(KC, 128)
            self.dma(b1, self.wtile(k + "w1", f, 128))
            self.dma(b3, self.wtile(k + "w3", f, 128))
            pA, pB = self.ps((f % 2) * 2), self.ps((f % 2) * 2 + 1)
            for kc in range(KC):
                self.mm(pA, b1[:, kc, :], self.hT[kc], kc == 0, kc == KC - 1)
            for kc in range(KC):
                self.mm(pB, b3[:, kc, :], self.hT[kc], kc == 0, kc == KC - 1)
            sil = self.sil[f % 2]
            self.act(sil, pA, AF.Silu)
            self.tt(self.uT[f], sil, pB, ALU.mult)
        for dc in range(KC):
            wb2 = self.w2b[dc % 2]
            self.dma(wb2, self.wtile(k + "w2", dc, 128))
            pO = self.ps(4 + dc % 2)
            for f in range(NF):
                self.mm(pO, wb2[:, f, :], self.uT[f], f == 0, f == NF - 1)
            self.stt(self.xres[dc], pO, 0.5, self.xres[dc], ALU.mult, ALU.add)

    def pipe(self, n, s1, s2):
        for i in range(n):
            s1(i)
            if i > 0:
                s2(i - 1)
        s2(n - 1)

    def rstd_from(self, ps2, ndim, parts=128):
        r = self.f32s[5 + self.rr("r", 2)]
        self.act(r[0:parts], ps2[0:parts], AF.Sqrt, bias=self.eps[0:parts], scale=1.0 / ndim)
        self.recip(r[0:parts], r[0:parts])
        return r

    def proj_qk_da(self, s):
        sl = slice(s * TT, (s + 1) * TT)
        kview = self.gview(self.gin[s], 0, R_KDA, 512, "(u d t) -> u d t", "kda", u=16, d=128)
        st = {}

        def s1(i):
            isq, u = i // 16, i % 16
            w = self.wbv(KC, 128)
            self.dma(w, self.wtile("win", (T_Q if isq == 0 else T_K) + u, 128))
            ps = self.ps(self.rr("pA", 4))
            for kc in range(KC):
                self.mm(ps, w[:, kc, :], self.hT[kc], kc == 0, kc == KC - 1)
            sq = self.sq16[self.rr("sq", 5)]
            self.act(sq, ps, AF.Square)
            st[i] = (ps, sq)

        def s2(i):
            isq, u = i // 16, i % 16
            ps, sq = st.pop(i)
            ps2 = self.ps(4 + self.rr("pB", 2))
            self.mm(ps2, self.ones_b, sq, True, True)
            r = self.rstd_from(ps2, 128)
            ob = self.ob[self.rr("ob", 6)]
            g = self.small[:, 0:1] if isq == 0 else self.cst[:, C_GKDA:C_GKDA + 1]
            self.stt(ob, ps, g, r, ALU.mult, ALU.mult)
            if isq == 0:
                self.dma(self.dv(self.qda_d, ("q", u, s), (u, slice(None), sl)), ob, eng="pool")
            else:
                self.dma(kview.v(kview.ap[u]), ob, eng="pool")

        self.pipe(32, s1, s2)

    def proj_v_da(self, s):
        vview = self.gview(self.gin[s], 0, R_VDA, 512, "(h t jl e) -> h t jl e", "vda", h=8, t=128, jl=4)
        for g in range(8):
            w = self.sb(self.B4 + 11264 * (g % 2), BF16, [KC, 256])
            self.dma(w, self.wtile("winv", g, 256))
            vb = self.vb[self.rr("vb", 3)]
            for tb in range(4):
                ps = self.ps(self.rr("pA", 4), 256)
                for kc in range(KC):
                    self.mm(ps, self.hT[kc][:, tb * 128:(tb + 1) * 128], w[:, kc, :], kc == 0, kc == KC - 1)
                self.act(vb[:, tb, :], ps, AF.Copy)
            self.dma(vview.v(vview.ap[g]), vb, eng="pool")

    def proj_gates(self, s):
        sl = slice(s * TT, (s + 1) * TT)
        for u in range(32):
            w = self.wbv(KC, 128)
            self.dma(w, self.wtile("win", T_GA + u, 128))
            ps = self.ps(self.rr("pA", 4))
            for kc in range(KC):
                self.mm(ps, w[:, kc, :], self.hT[kc], kc == 0, kc == KC - 1)
            ob = self.ob[self.rr("ob", 6)]
            self.act(ob, ps, AF.Sigmoid)
            self.dma(self.dv(self.gT_d, ("g", u, s), (u, slice(None), sl)), ob, eng="pool")

    def latent(self, tbase, nch, gcol, f32buf, nbuf):
        ps2 = self.ps(6 + self.rr("pC", 2))
        for i in range(nch):
            w = self.wbv(KC, 128)
            self.dma(w, self.wtile("win", tbase + i, 128))
            ps = self.ps(self.rr("pA", 4))
            for kc in range(KC):
                self.mm(ps, w[:, kc, :], self.hT[kc], kc == 0, kc == KC - 1)
            self.act(f32buf[i], ps, AF.Copy)
            sq = self.sq16[self.rr("sq", 5)]
            self.act(sq, ps, AF.Square)
            self.mm(ps2, self.ones_b, sq, i == 0, i == nch - 1)
        r = self.rstd_from(ps2, nch * 128)
        for i in range(nch):
            self.stt(nbuf[i], f32buf[i], self.cst[:, gcol + i:gcol + i + 1], r, ALU.mult, ALU.mult)

    def rope(self, xr, out_bf):
        psr = self.ps(4 + self.rr("pB", 2), 512, 64)
        self.mm(psr, self.rmat, xr, True, True)
        t1, t2 = self.f32s[7][0:64], self.f32s[8][0:64]
        self.tt(t1, xr, self.cos2, ALU.mult)
        self.tt(t2, psr, self.sin2, ALU.mult)
        self.tt(out_bf, t1, t2, ALU.add)

    def proj_mla(self, s):
        sl = slice(s * TT, (s + 1) * TT)
        U = self.B2
        cqf = [self.sb(U + 2048 * i, F32, [TT]) for i in range(6)]
        cqn = [self.sb(U + 12288 + 1024 * i, BF16, [TT]) for i in range(6)]
        ckf = [self.sb(U + 18432 + 2048 * i, F32, [TT]) for i in range(4)]
        ckn = [self.sb(U + 26624 + 1024 * i, BF16, [TT]) for i in range(4)]
        krb = self.sb(U + 30720, F32, [TT], 64)
        sqkr = self.sb(U + 32768, BF16, [TT], 64)
        xrq = self.sb(U + 34816, F32, [TT], 64)
        krg = self.sb(U + 36864, F32, [TT], 64)
        self.dma(self.cos2, Buf(self.rope_d.ap[:, 0, sl], self.rope_d.ts))
        self.dma(self.sin2, Buf(self.rope_d.ap[:, 1, sl], self.rope_d.ts))
        self.dma(self.sinS, Buf(self.rope_d.ap[:, 2, sl], self.rope_d.ts))
        self.latent(T_CQ, 6, C_GQA, cqf, cqn)
        st = {}

        cg, sg = self.f32s[7][0:64], self.f32s[8][0:64]
        self.ts(cg, self.cos2, self.small[0:64, 2:3], ALU.mult)
        self.ts(sg, self.sinS, self.small[0:64, 6:7], ALU.mult)

        def q1(h):
            w = self.wbv(6, 256)
            self.dma(w, self.wtile("wqb", h, 256))
            psn = self.ps(self.rr("pA", 4))
            psr = self.ps(self.rr("pA", 4), 512, 64)
            psw = self.ps(4 + self.rr("pB", 2), 512, 64)
            for kc in range(6):
                self.mm(psn, w[:, kc, 0:128], cqn[kc], kc == 0, kc == 5)
            for kc in range(6):
                self.mm(psr, w[:, kc, 128:192], cqn[kc], kc == 0, kc == 5)
            for kc in range(6):
                self.mm(psw, w[:, kc, 192:256], cqn[kc], kc == 0, kc == 5)
            sqn = self.sq16[self.rr("sq", 5)]
            sqr = self.sq16[self.rr("sq", 5)]
            self.act(sqn, psn, AF.Square)
            self.act(sqr[0:64], psr, AF.Square)
            st[h] = (psn, psr, psw, sqn, sqr)

        def q2(h):
            psn, psr, psw, sqn, sqr = st.pop(h)
            ps2 = self.ps(6 + self.rr("pC", 2))
            self.mm(ps2, self.ones_b, sqn, True, False)
            self.mm(ps2, self.ones_b[0:64, :], sqr[0:64], False, True)
            r = self.rstd_from(ps2, 192)
            ob = self.ob[self.rr("ob", 6)]
            self.stt(ob, psn, self.small[:, 1:2], r, ALU.mult, ALU.mult)
            self.dma(self.dv(self.qmn_d, ("q", h, s), (h, slice(None), sl)), ob, eng="pool")
            self.tt(xrq, psr, cg, ALU.mult)
            self.tt(krg, psw, sg, ALU.mult)
            self.tt(xrq, xrq, krg, ALU.add)
            ob2 = self.ob[self.rr("ob", 6)]
            self.tt(ob2[0:64], xrq, r[0:64], ALU.mult)
            self.dma(self.dv(self.qmr_d, ("q", h, s), (h, slice(None), sl)), ob2[0:64], eng="pool")

        self.pipe(16, q1, q2)
        self.latent(T_CKV, 4, C_GKVA, ckf, ckn)
        w = self.wbv(KC, 128)
        self.dma(w, self.wtile("win", T_KR, 128))
        pskr = self.ps(self.rr("pA", 4), 512, 64)
        for kc in range(KC):
            self.mm(pskr, w[:, kc, 0:64], self.hT[kc], kc == 0, kc == KC - 1)
        self.act(sqkr, pskr, AF.Square)
        self.ts(krg, pskr, self.cst[0:64, C_GKMR:C_GKMR + 1], ALU.mult)
        self.rope(krg, krb)
        knview = self.gview(self.gin[s], 0, R_KMN, 512, "(u d t) -> u d t", "kmn", u=16, d=128)
        krview = self.gview(self.gin[s], 0, R_KMR, 256, "(u d t) -> u d t", "kmr", u=16, d=64)

        def k1(h):
            w = self.wbv(4, 128)
            self.dma(w, self.wtile("wkvbk", h, 128))
            psn = self.ps(self.rr("pA", 4))
            for kc in range(4):
                self.mm(psn, w[:, kc, :], ckn[kc], kc == 0, kc == 3)
            sqn = self.sq16[self.rr("sq", 5)]
            self.act(sqn, psn, AF.Square)
            st[h] = (psn, sqn)

        def k2(h):
            psn, sqn = st.pop(h)
            ps2 = self.ps(6 + self.rr("pC", 2))
            self.mm(ps2, self.ones_b, sqn, True, False)
            self.mm(ps2, self.ones_b[0:64, :], sqkr, False, True)
            r = self.rstd_from(ps2, 192)
            ob = self.ob[self.rr("ob", 6)]
            self.stt(ob, psn, self.cst[:, C_GKMN:C_GKMN + 1], r, ALU.mult, ALU.mult)
            self.dma(knview.v(knview.ap[h]), ob, eng="pool")
            ob2 = self.ob[self.rr("ob", 6)]
            self.tt(ob2[0:64], krb, r[0:64], ALU.mult)
            self.dma(krview.v(krview.ap[h]), ob2[0:64], eng="pool")

        self.pipe(16, k1, k2)
        vview = self.gview(self.gin[s], 0, R_VM, 512, "(h t jl e) -> h t jl e", "vm", h=16, t=128, jl=4)
        for g in range(8):
            w = self.wbv(4, 256)
            self.dma(w, self.wtile("wkvbv", g, 256))
            vb = self.vb[self.rr("vb", 3)]
            for tb in range(4):
                ps = self.ps(self.rr("pA", 4), 256)
                for kc in range(4):
                    self.mm(ps, ckn[kc][:, tb * 128:(tb + 1) * 128], w[:, kc, :], kc == 0, kc == 3)
                self.act(vb[:, tb, :], ps, AF.Copy)
            for hl in range(2):
                self.dma(vview.v(vview.ap[2 * g + hl]), vb[:, :, hl * 128:(hl + 1) * 128], eng="pool")

    def phase1(self, s):
        sl = slice(s * TT, (s + 1) * TT)
        for kc in range(KC):
            self.dma(self.xres[kc], self.xT.v(self.xT.ap[kc * 128:(kc + 1) * 128, sl]))
        self.rmsnorm(C_GF1, 6 + s % 2)
        self.ffn("f1")
        for kc in range(KC):
            self.dma(self.dv(self.x1_d, ("x1", kc, s), (kc, slice(None), sl)), self.xres[kc], eng="pool")
        self.rmsnorm(C_GMIX, 6 + (s + 1) % 2)
        self.proj_qk_da(s)
        self.proj_v_da(s)
        self.proj_gates(s)
        self.proj_mla(s)
        keys = ["kda", "vda", "kmn", "kmr", "vm"]
        src = Buf(self.gin[s].ap(), [self.dT((self.gin[s].name, k)) for k in keys])
        dst = Buf(self.gout[s].ap(), [self.dT((self.gout[s].name, "all"))])
        self.allgather(dst, src)

    def load_kv(self, dst, s_pair, row0, nrows, pattern, idx, sizes, eng="sp"):
        for half in range(2):
            go = self.gout[s_pair + half]
            for r in range(8):
                v = self.gview(go, r, row0, nrows, pattern, "all", **sizes)
                src = v.v(v.ap[idx])
                p0 = 32 * half + 4 * r
                self.dma(dst[:, p0:p0 + 4, :], src, eng=eng)

    def attention(self, b):
        A = self.B0
        sb = self.sb
        vbuf = [sb(A + 32768 * i, BF16, [64, 256]) for i in range(2)]
        kbuf = [sb(A + 65536 + 16384 * i, BF16, [64, 128]) for i in range(2)]
        krbuf = [sb(A + 98304 + 16384 * i, BF16, [64, 128], 64) for i in range(2)]
        Q = A + 131072
        qn = [sb(Q + 2048 * i, BF16, [1024]) for i in range(2)]
        qr = [sb(Q + 4096 + 2048 * i, BF16, [1024], 64) for i in range(2)]
        pt = [sb(Q + 8192 + 1024 * i, BF16, [TT]) for i in range(4)]
        tmpb = [sb(Q + 34816 + 512 * i, F32, [128]) for i in range(3)]
        o1 = [[sb(Q + 13312 + 4096 * q + 2048 * e, F32, [TT]) for e in range(2)] for q in range(2)]
        rinv = sb(Q + 21504, F32, [TT])
        tf = sb(Q + 23552, F32, [TT])
        sqb = [sb(Q + 25600 + 2048 * i, BF16, [TT]) for i in range(2)]
        rsub = sb(Q + 29696, F32, [TT])
        yb = [sb(Q + 31744 + 1024 * i, BF16, [TT]) for i in range(3)]
        tmpf = [sb(Q + 36352 + 2048 * i, F32, [TT]) for i in range(3)]
        assert Q + 42496 <= self.arena_bytes
        sp = 2 * b
        col0 = b * 1024
        neglam = self.small[:, 5:6]

        def kloop(qh, kb, krb_, qnb, qrb, vb_, nec, banks, da=None, G=7):
            psO = [self.ps(banks[e]) for e in range(nec)]
            psS_ = self.ps(banks[2])
            jA, jB = 4 * qh, 4 * qh + 3
            g0 = max(0, jA - G)
            steps = []
            for g in range(g0, jB + 1):
                jlo, jhi = max(g, jA), min(g + G, jB)
                if jlo <= jhi:
                    for mp in range(8):
                        steps.append((8 * g + mp, g, mp, jlo, jhi))
            first_g = {j: max(g0, j - G) for j in range(jA, jB + 1)}
            stA = {}

            def qk(i):
                m, g, mp, jlo, jhi = steps[i]
                n = 128 * (jhi - jlo + 1)
                q0 = 128 * jlo
                ps = self.ps(self.rr("pS", 3), n)
                km = kpos(m)
                if krb_ is None:
                    self.mm(ps, kb[:, km, :], qnb[:, q0:q0 + n], True, True)
                else:
                    self.mm(ps, kb[:, km, :], qnb[:, q0:q0 + n], True, False)
                    self.mm(ps, krb_[:, km, :], qrb[:, q0:q0 + n], False, True)
                p = pt[self.rr("pt", 4)]
                if da is not None:
                    h, slope = da
                    t = tmpf[self.rr("tmpf", 3)]
                    if jlo == g:
                        self.stt(t[:, 0:128], self.acp[:, mp, :], slope, ps[:, 0:128], ALU.mult, ALU.add)
                        if n > 128:
                            self.stt(t[:, 128:n], self.strip[:, 128:n], slope, ps[:, 128:n], ALU.mult, ALU.add)
                    else:
                        off = 128 * (jlo - g)
                        self.stt(t[:, 0:n], self.strip[:, off:off + n], slope, ps[:, 0:n], ALU.mult, ALU.add)
                    self.act(p[:, 0:n], t[:, 0:n], AF.Exp, bias=self.bpp[:, h, mp:mp + 1])
                else:
                    a = 0
                    if jlo == g:
                        tb_ = tmpb[self.rr("tmpb", 3)]
                        self.tt(tb_, ps[:, 0:128], self.mmk[:, mp * 128:(mp + 1) * 128], ALU.add)
                        self.act(p[:, 0:128], tb_, AF.Exp)
                        a = 128
                    if n > a:
                        self.act(p[:, a:n], ps[:, a:n], AF.Exp)
                stA[i] = (p, n, 128 * (jlo - jA), km)

            def pv(i):
                m, g, mp, jlo, jhi = steps[i]
                p, n, c0, km = stA.pop(i)
                for e in range(nec):
                    self.mm(psO[e][:, c0:c0 + n], vb_[:, km, e * 128:(e + 1) * 128], p[:, 0:n], False, False)
                self.mm(psS_[:, c0:c0 + n], self.ones_b, p[:, 0:n], False, False)

            for e in range(nec):
                self.mm(psO[e], self.ones_b, self.zeros_b, True, False)
            self.mm(psS_, self.ones_b, self.zeros_b, True, False)
            LA = 2
            ns = len(steps)
            for i in range(ns + LA):
                if i < ns:
                    qk(i)
                if i >= LA:
                    pv(i - LA)
            return psO, psS_

        for h in range(8):
            vb_ = vbuf[self.rr("vbuf", 2)]
            self.load_kv(vb_, sp, R_VDA, 512, "(h t jl e) -> h t jl e", h, dict(h=8, t=128, jl=4))
            for mp_ in range(2):
                u = 2 * h + mp_
                kb = kbuf[self.rr("kbuf", 2)]
                self.load_kv(kb, sp, R_KDA, 512, "(u d t) -> u d t", u, dict(u=16, d=128))
                qnb = qn[self.rr("qn", 2)]
                self.dma(qnb, Buf(self.qda_d[u, :, col0:col0 + 1024], [self.dT((self.qda_d.name, ("q", u, 2 * b))),
                                                                       self.dT((self.qda_d.name, ("q", u, 2 * b + 1)))]))
                for qh in range(2):
                    par = self.rr("oset", 2)
                    banks = (3, 4, 5) if par == 0 else (6, 7, 5)
                    psO, psS_ = kloop(qh, kb, None, qnb, None, vb_, 2, banks, da=(h, SLOPES[h]), G=GKEEP[h])
                    self.recip(rinv, psS_)
                    if mp_ == 0:
                        for e in range(2):
                            self.tt(o1[qh][e], psO[e], rinv, ALU.mult)
                    else:
                        for e in range(2):
                            self.tt(tf, psO[e], rinv, ALU.mult)
                            self.stt(o1[qh][e], tf, neglam, o1[qh][e], ALU.mult, ALU.add)
                        ps2 = self.ps(self.rr("pS", 3))
                        for e in range(2):
                            self.act(sqb[e], o1[qh][e], AF.Square)
                            self.mm(ps2, self.ones_b, sqb[e], e == 0, e == 1)
                        self.act(rsub, ps2, AF.Sqrt, bias=self.eps, scale=1.0 / 256)
                        self.recip(rsub, rsub)
                        for e in range(2):
                            y = yb[self.rr("yb", 3)]
                            self.stt(y, o1[qh][e], self.small[:, 3 + e:4 + e], rsub, ALU.mult, ALU.mult)
                            c = col0 + qh * TT
                            self.dma(self.dv(self.yT_d, ("y", 2 * h + e, 2 * b + qh), (2 * h + e, slice(None), slice(c, c + TT))), y, eng="pool")
        for h in range(16):
            vb_ = vbuf[self.rr("vbuf", 2)]
            vbm = vb_.v(vb_.ap[:, :, 0:128])
            self.load_kv(vbm, sp, R_VM, 512, "(h t jl e) -> h t jl e", h, dict(h=16, t=128, jl=4))
            kb = kbuf[self.rr("kbuf", 2)]
            self.load_kv(kb, sp, R_KMN, 512, "(u d t) -> u d t", h, dict(u=16, d=128))
            krb_ = krbuf[self.rr("krbuf", 2)]
            self.load_kv(krb_, sp, R_KMR, 256, "(u d t) -> u d t", h, dict(u=16, d=64))
            qnb = qn[self.rr("qn", 2)]
            qrb = qr[self.rr("qr", 2)]
            ts_ = [self.dT((self.qmn_d.name, ("q", h, 2 * b))), self.dT((self.qmn_d.name, ("q", h, 2 * b + 1)))]
            self.dma(qnb, Buf(self.qmn_d[h, :, col0:col0 + 1024], ts_))
            ts_ = [self.dT((self.qmr_d.name, ("q", h, 2 * b))), self.dT((self.qmr_d.name, ("q", h, 2 * b + 1)))]
            self.dma(qrb, Buf(self.qmr_d[h, :, col0:col0 + 1024], ts_))

            for qh in range(2):
                par = self.rr("oset", 2)
                banks = (3, 4, 5) if par == 0 else (6, 7, 5)
                psO, psS_ = kloop(qh, kb, krb_, qnb, qrb, vbm, 1, banks)
                self.recip(rinv, psS_)
                y = yb[self.rr("yb", 3)]
                self.tt(y, psO[0], rinv, ALU.mult)
                c = col0 + qh * TT
                self.dma(self.dv(self.yT_d, ("y", 16 + h, 2 * b + qh), (16 + h, slice(None), slice(c, c + TT))), y, eng="pool")

    def phase3(self, s):
        sl = slice(s * TT, (s + 1) * TT)
        U = self.B2
        ya = [self.sb(U + 1024 * i, BF16, [TT]) for i in range(16)]
        yb_ = [self.sb(U + 16384 + 1024 * i, BF16, [TT]) for i in range(16)]
        for i in range(16):
            self.dma(ya[i], self.dv(self.yT_d, ("y", i, s), (i, slice(None), sl)))
            self.dma(yb_[i], self.dv(self.yT_d, ("y", 16 + i, s), (16 + i, slice(None), sl)))
        for oc in range(16):
            wa_, wb_ = self.wbv(KC, 128), self.wbv(KC, 128)
            self.dma(wa_, self.wtile("wa", oc, 128))
            self.dma(wb_, self.wtile("wb", oc, 128))
            ga, gb = self.ob[self.rr("ob", 6)], self.ob[self.rr("ob", 6)]
            self.dma(ga, self.dv(self.gT_d, ("g", oc, s), (oc, slice(None), sl)))
            self.dma(gb, self.dv(self.gT_d, ("g", 16 + oc, s), (16 + oc, slice(None), sl)))
            pa, pb = self.ps(self.rr("pA", 4)), self.ps(self.rr("pA", 4))
            for kc in range(KC):
                self.mm(pa, wa_[:, kc, :], ya[kc], kc == 0, kc == KC - 1)
            for kc in range(KC):
                self.mm(pb, wb_[:, kc, :], yb_[kc], kc == 0, kc == KC - 1)
            m1 = self.f32s[self.rr("sq", 5)]
            self.tt(m1, pa, ga, ALU.mult)
            m2 = self.f32s[self.rr("sq", 5)]
            self.tt(m2, pb, gb, ALU.mult)
            self.tt(self.hT[oc], m1, m2, ALU.add)
        for kc in range(KC):
            self.dma(self.xres[kc], self.dv(self.x1_d, ("x1", kc, s), (kc, slice(None), sl)))
        for oc in range(16):
            w = self.wbv(KC, 128)
            self.dma(w, self.wtile("wo", oc, 128))
            ps = self.ps(4 + self.rr("pB", 2))
            for kc in range(KC):
                self.mm(ps, w[:, kc, :], self.hT[kc], kc == 0, kc == KC - 1)
            self.tt(self.xres[oc], ps, self.xres[oc], ALU.add)
        self.rmsnorm(C_GF2, 6 + s % 2)
        self.ffn("f2")
        for kc in range(KC):
            self.dma(self.out_d.v(self.out_d.ap[kc * 128:(kc + 1) * 128, sl]), self.xres[kc], eng="pool")

    def build(self):
        self.setup()
        self.consts()
        first = ["f1w1", "f1w3", "f1w2", "win", "winv", "wqb", "wkvbk", "wkvbv"]
        rest = ["wa", "wb", "wo", "f2w1", "f2w3", "f2w2"]
        for n in first:
            self.convert([n])
            self.gather_w([n])
        self.convert(rest)
        for s in range(NSLOT):
            self.phase1(s)
            if s == 0:
                self.gather_w(rest)
        for b in range(2):
            self.attention(b)
        for s in range(NSLOT):
            self.phase3(s)
        if self.stage == 50:
            for hnd, dt in [(self.x1_d, F32), (self.qda_d, BF16), (self.qmn_d, BF16), (self.qmr_d, BF16), (self.gT_d, BF16),
                            (self.yT_d, BF16), (self.gin[0], BF16), (self.gin[3], BF16), (self.gout[1], BF16)]:
                o = self.dram_out("dbg_" + hnd.name, list(hnd.shape), dt)
                ts_ = [t for k, t in self.dram.items() if isinstance(k, tuple) and k[0] == hnd.name]
                self.dma(o, Buf(hnd.ap(), ts_))
        return self.finish()

    def finish(self):
        nc = self.nc
        sems = self.p.finalize()
        semh = {}
        for i, s in enumerate(sems):
            semh[s] = self.es.enter_context(nc.semaphore("s%d" % i))
        finals = {}
        for e in ENGS:
            for ins in self.p.streams[e]:
                if ins.kind != "c":
                    finals[ins.sem] = max(finals.get(ins.sem, 0), ins.val)
        fw = sorted(finals.items())
        p = self.p
        with nc.Block() as block:
            @block.tensor
            def _(e):
                p.replay("pe", e, semh)

            @block.scalar
            def _(e):
                p.replay("act", e, semh)

            @block.vector
            def _(e):
                p.replay("dve", e, semh)

            @block.gpsimd
            def _(e):
                p.replay("pool", e, semh)

            @block.sync
            def _(e):
                p.replay("sp", e, semh, final_waits=fw)
        self.es.close()
        return nc


def _tok_index(c):
    idx = np.empty((8, 128), np.int64)
    for j in range(8):
        idx[j, :] = (8 * j + c) * 128 + np.arange(128)
    return idx.reshape(1024)


def _consts(c, inp):
    cst = np.zeros((128, CST_W), np.float32)
    cst[:, C_ONES:C_ONES + 128] = 1.0
    for i in range(32):
        cst[i + 32, C_RMAT + i] = -1.0
        cst[i, C_RMAT + 32 + i] = 1.0

    def fm(v, n):
        return np.asarray(v, np.float32).reshape(n, 128).T
    cst[:, C_GF1:C_GF1 + 16] = fm(inp["ffn1_norm_g"][0], 16)
    cst[:, C_GMIX:C_GMIX + 16] = fm(inp["mix_norm_g"][0], 16)
    cst[:, C_GF2:C_GF2 + 16] = fm(inp["ffn2_norm_g"][0], 16)
    cst[:, C_GQA:C_GQA + 6] = fm(inp["mla_q_a_norm_g"][0], 6)
    cst[:, C_GKVA:C_GKVA + 4] = fm(inp["mla_kv_a_norm_g"][0], 4)
    cst[:, C_GQDA] = inp["da_q_norm_g"][0]
    cst[:, C_GKDA] = inp["da_k_norm_g"][0]
    cst[:, C_GQMN] = inp["mla_q_norm_g"][0][:128]
    cst[:64, C_GQMR] = inp["mla_q_norm_g"][0][128:]
    cst[:32, C_GQMRW] = inp["mla_q_norm_g"][0][160:]
    cst[32:64, C_GQMRW] = inp["mla_q_norm_g"][0][128:160]
    cst[:, C_GKMN] = inp["mla_k_norm_g"][0][:128]
    cst[:64, C_GKMR] = inp["mla_k_norm_g"][0][128:]
    cst[:, C_SUB0] = inp["da_subln_g"][0][:128]
    cst[:, C_SUB1] = inp["da_subln_g"][0][128:]
    cst[:, C_EPS] = EPS
    for i, n in enumerate(["da_lambda_q1", "da_lambda_k1", "da_lambda_q2", "da_lambda_k2"]):
        cst[:, C_LAM + i] = inp[n][0]
    p = np.arange(128, dtype=np.float64)
    for m in range(8):
        cst[:, C_A2 + m] = p - 64 - 128 * (c - m)
    f = np.arange(128, dtype=np.float64)
    chunk_ok = (p[:, None] // 64) <= (f[None, :] // 64)
    acp = np.zeros((128, 8, 128), np.float32)
    for m in range(8):
        if m < c:
            mk = np.zeros((128, 128))
        elif m == c:
            d2 = 2.0 * (f[None, :] - p[:, None])
            acp[:, m, :] = np.where(chunk_ok, np.minimum(d2, 0.0), NEG)
            mk = np.where(chunk_ok, 0.0, -30000.0)
        else:
            acp[:, m, :] = NEG
            mk = np.full((128, 128), -30000.0)
        cst[:, C_MM + m * 128:C_MM + (m + 1) * 128] = mk
    strip = np.broadcast_to(-1024.0 * (np.arange(1024) // 128)[None, :], (128, 1024)).astype(np.float32)
    packed = np.ascontiguousarray(np.concatenate([acp.reshape(128, 1024), strip], axis=1).astype(ml_dtypes.bfloat16))
    cst[:, C_AC:C_AC + 1024] = packed.view(np.uint16).reshape(128, 2048).view(np.uint32).view(np.float32)
    return cst


def _rope_tab(c):
    pos = np.concatenate([_tok_index(c), _tok_index(c)]).astype(np.float32)
    inv = (10000.0 ** (-np.arange(0, 64, 2, dtype=np.float32) / 64)).astype(np.float32)
    ang = pos[None, :] * inv[:, None]
    cs, sn = np.cos(ang).astype(np.float32), np.sin(ang).astype(np.float32)
    tab = np.empty((64, 3, NTOK), np.float32)
    tab[:32, 0], tab[32:, 0] = cs, cs
    tab[:32, 1], tab[32:, 1] = sn, sn
    tab[:32, 2], tab[32:, 2] = -sn, sn
    return tab


def _fm_tiles(W, c0, ntile, M, mt, stride=None):
    K = W.shape[0]
    kcn = K // 128
    stride = M if stride is None else stride
    out = np.zeros((ntile, 128, kcn, mt), np.float32)
    for i in range(ntile):
        blk = W[:, c0 + i * stride:c0 + i * stride + M].reshape(kcn, 128, M)
        out[i, :, :, :M] = blk.transpose(1, 0, 2)
    return out.reshape(ntile, 128, kcn * mt)


def _weight_groups(inp):
    g = lambda n: np.asarray(inp[n], np.float32)[0]
    W = {}
    for k, n in (("f1", "ffn1"), ("f2", "ffn2")):
        W[k + "w1"] = _fm_tiles(g(n + "_w1"), 0, 44, 128, 128)
        W[k + "w3"] = _fm_tiles(g(n + "_w3"), 0, 44, 128, 128)
        W[k + "w2"] = _fm_tiles(g(n + "_w2"), 0, 16, 128, 128)
    win = g("w_in")
    parts = [_fm_tiles(win, 0, 16, 128, 128), _fm_tiles(win, 2048, 16, 128, 128), _fm_tiles(win, 6144, 6, 128, 128),
             _fm_tiles(win, 6912, 4, 128, 128), _fm_tiles(win, 7424, 1, 64, 128), _fm_tiles(win, 7488, 32, 128, 128)]
    W["win"] = np.concatenate(parts, 0)
    W["winv"] = _fm_tiles(win, 4096, 8, 256, 256)
    wq = g("mla_w_qb").reshape(768, 16, 192)
    wq = np.concatenate([wq, wq[:, :, 160:192], wq[:, :, 128:160]], axis=2).reshape(768, 16 * 256)
    W["wqb"] = _fm_tiles(wq, 0, 16, 256, 256)
    kvb = g("mla_w_kvb")
    W["wkvbk"] = _fm_tiles(kvb, 0, 16, 128, 128, stride=256)
    v = np.zeros((8, 128, 4, 256), np.float32)
    for gi in range(8):
        for hl in range(2):
            h = 2 * gi + hl
            v[gi, :, :, hl * 128:(hl + 1) * 128] = kvb[:, h * 256 + 128:h * 256 + 256].reshape(4, 128, 128).transpose(1, 0, 2)
    W["wkvbv"] = v.reshape(8, 128, 1024)
    W["wa"] = _fm_tiles(g("w_branch_a"), 0, 16, 128, 128)
    W["wb"] = _fm_tiles(g("w_branch_b"), 0, 16, 128, 128)
    W["wo"] = _fm_tiles(g("w_out"), 0, 16, 128, 128)
    out = {}
    for n, nt, e in WG:
        a = W[n]
        if a.shape[0] < nt:
            a = np.concatenate([a, np.zeros((nt - a.shape[0], 128, e), np.float32)], 0)
        assert a.shape == (nt, 128, e), (n, a.shape)
        out[n] = a.reshape(8, nt // 8 * 128, e)
    return out


def make_in_maps(inp):
    x = np.asarray(inp["x"], np.float32)
    wgs = _weight_groups(inp)
    maps = []
    for c in range(NCORES):
        ti = _tok_index(c)
        xl = np.concatenate([x[0, ti], x[1, ti]], axis=0)
        m = {"xT": np.ascontiguousarray(xl.T), "cst": _consts(c, inp), "rope": _rope_tab(c)}
        for n, _, _ in WG:
            m["ws_" + n] = np.ascontiguousarray(wgs[n][c])
        maps.append(m)
    return maps


def assemble(results):
    out = np.empty((2, SEQ, D), np.float32)
    for c in range(NCORES):
        ti = _tok_index(c)
        o = np.asarray(results[c]["outT"]).T
        out[0, ti] = o[:1024]
        out[1, ti] = o[1024:]
    return out


def kernel(**inputs):
    nc = Kern(99).build()
    res = run_bass_kernel_spmd(nc, make_in_maps(inputs), core_ids=list(range(NCORES)))
    return assemble(res.results)
```

```python
import math
from contextlib import ExitStack

import numpy as np
import ml_dtypes
import concourse.bass as bass
import concourse.mybir as mybir
from concourse.bass_utils import run_bass_kernel_spmd

F32 = mybir.dt.float32
BF16 = mybir.dt.bfloat16
AF = mybir.ActivationFunctionType
ALU = mybir.AluOpType

NCORES = 8
D = 2048
DFF = 5632
NF = DFF // 128
KC = D // 128
SEQ = 8192
NTOK = 2048
TT = 512
NSLOT = 4
EPS = 1e-6
LAMBDA_INIT = 0.8 - 0.6 * math.exp(-0.3 * 0)
NEG = -1.0e6

ENGS = ("pe", "act", "dve", "pool", "sp")
NDMASEM = {"sp": 40, "pool": 24, "act": 8}


class T:
    __slots__ = ("w", "r", "rd")

    def __init__(self):
        self.w = None
        self.r = {}
        self.rd = []


class Buf:
    __slots__ = ("ap", "ts")

    def __init__(self, ap, ts):
        self.ap = ap
        self.ts = ts

    def __getitem__(self, idx):
        return Buf(self.ap[idx], self.ts)

    def v(self, ap):
        return Buf(ap, self.ts)


class Ins:
    __slots__ = ("eng", "fn", "waits", "signal", "sem", "val", "kind", "prev")

    def __init__(self, eng, fn, kind):
        self.eng = eng
        self.fn = fn
        self.kind = kind
        self.waits = []
        self.signal = False
        self.sem = None
        self.val = 0
        self.prev = None


class Prog:
    def __init__(self):
        self.streams = {e: [] for e in ENGS}

    def emit(self, eng, fn, reads=(), writes=(), kind="c"):
        ins = Ins(eng, fn, kind)
        deps = {}
        true_deps = set()
        for b in reads:
            for t in b.ts:
                if t.w is not None:
                    deps[id(t.w)] = t.w
                    true_deps.add(id(t.w))
        for b in writes:
            for t in b.ts:
                if t.w is not None:
                    deps[id(t.w)] = t.w
                    true_deps.add(id(t.w))
                for r in t.r.values():
                    deps[id(r)] = r
                for r in t.rd:
                    deps[id(r)] = r
        for d in deps.values():
            if d.kind == "c" and d.eng == eng and kind == "c":
                if eng == "pe" or id(d) not in true_deps:
                    continue
            d.signal = True
            ins.waits.append(d)
        for b in reads:
            for t in b.ts:
                if kind == "c":
                    t.r[eng] = ins
                else:
                    t.rd.append(ins)
        for b in writes:
            for t in b.ts:
                t.w = ins
                t.r = {}
                t.rd = []
        self.streams[eng].append(ins)
        return ins

    def finalize(self):
        sems = set()
        for e in ENGS:
            cnt, semi, k, ncc = 0, 0, 0, 0
            P = NDMASEM.get(e, 1)
            for ins in self.streams[e]:
                if ins.kind == "c":
                    if ins.signal:
                        cnt += 1
                        if cnt > 30000:
                            semi += 1
                            cnt = 1
                        ins.sem = ("c", e, semi)
                        ins.val = cnt
                        sems.add(ins.sem)
                elif ins.kind == "d":
                    ins.sem = ("d", e, k % P)
                    ins.val = 16 * (k // P + 1)
                    if k >= P:
                        ins.prev = (ins.sem, 16 * (k // P))
                    k += 1
                    sems.add(ins.sem)
                else:
                    ins.sem = ("cc", e, ncc)
                    ins.val = 1
                    ncc += 1
                    sems.add(ins.sem)
        return sorted(sems)

    def replay(self, eng, e, semh, final_waits=None):
        waited = {}

        def w(sem, val):
            if waited.get(sem, 0) < val:
                e.wait_ge(semh[sem], val)
                waited[sem] = val

        for ins in self.streams[eng]:
            for d in ins.waits:
                w(d.sem, d.val)
            if ins.prev is not None:
                w(*ins.prev)
            r = ins.fn(e)
            if ins.kind == "d":
                r.then_inc(semh[ins.sem], 16)
            elif ins.kind == "cc":
                r.then_inc(semh[ins.sem], 1)
            elif ins.signal:
                r.then_inc(semh[ins.sem], 1)
        if final_waits:
            for sem, val in final_waits:
                w(sem, val)


class Builder:
    def __init__(self, stage):
        self.stage = stage
        self.nc = bass.Bass("TRN2", target_bir_lowering=False)
        self.p = Prog()
        self.es = ExitStack()
        self.sb_off = 0
        self.dram = {}

    def dram_in(self, name, shape, dt=F32):
        h = self.nc.dram_tensor(name, list(shape), dt, kind="ExternalInput")
        b = Buf(h.ap(), [T()])
        self.dram[name] = b
        return b

    def dram_out(self, name, shape, dt=F32):
        h = self.nc.dram_tensor(name, list(shape), dt, kind="ExternalOutput")
        b = Buf(h.ap(), [T()])
        self.dram[name] = b
        return b

    def dram_tmp(self, name, shape, dt=BF16):
        h = self.nc.dram_tensor(name, list(shape), dt)
        return h

    def sbuf_init(self, nbytes):
        self.arena = self.es.enter_context(self.nc.sbuf_tensor("arena", [128, nbytes // 4], F32))
        self.arena_bytes = nbytes
        self.pages = [T() for _ in range(nbytes // 512)]
        self.psum = self.es.enter_context(self.nc.psum_tensor("psum", [128, 8 * 512], F32))
        self.pspages = [T() for _ in range(8 * 4)]

    def sb(self, off, dt, free_shape, parts=128):
        esz = 4 if dt == F32 else 2
        n = int(np.prod(free_shape))
        assert off % 4 == 0 and off + n * esz <= self.arena_bytes, (off, n, esz)
        base = self.arena.bitcast(dt) if dt != F32 else self.arena
        ap = base[0:parts, off // esz: off // esz + n]
        if len(free_shape) == 2:
            ap = ap.rearrange("p (a b) -> p a b", a=free_shape[0])
        elif len(free_shape) == 3:
            ap = ap.rearrange("p (a b c) -> p a b c", a=free_shape[0], b=free_shape[1])
        ts = self.pages[off // 512: (off + n * esz + 511) // 512]
        return Buf(ap, ts)

    def ps(self, bank, cols=512, parts=128, coff=0):
        ap = self.psum[0:parts, bank * 512 + coff: bank * 512 + coff + cols]
        ts = self.pspages[bank * 4 + coff // 128: bank * 4 + (coff + cols + 127) // 128]
        return Buf(ap, ts)

    def dma(self, out, in_, eng="sp"):
        o, i = out.ap, in_.ap
        return self.p.emit(eng, lambda e: e.dma_start(out=o, in_=i), reads=[in_], writes=[out], kind="d")

    def mm(self, out, lhsT, rhs, start, stop):
        o, l, r = out.ap, lhsT.ap, rhs.ap
        return self.p.emit("pe", lambda e: e.matmul(o, l, r, start=start, stop=stop, skip_group_check=True),
                           reads=[lhsT, rhs], writes=[out])

    def act(self, out, in_, func, bias=None, scale=None, eng="act"):
        o, i = out.ap, in_.ap
        kw = {}
        rd = [in_]
        if bias is not None:
            if isinstance(bias, Buf):
                kw["bias"] = bias.ap
                rd.append(bias)
            else:
                kw["bias"] = bias
        if scale is not None:
            if isinstance(scale, Buf):
                kw["scale"] = scale.ap
                rd.append(scale)
            else:
                kw["scale"] = scale
        return self.p.emit(eng, lambda e: e.activation(out=o, in_=i, func=func, **kw), reads=rd, writes=[out])

    def tt(self, out, a, b, op, eng="dve"):
        o, x, y = out.ap, a.ap, b.ap
        return self.p.emit(eng, lambda e: e.tensor_tensor(out=o, in0=x, in1=y, op=op), reads=[a, b], writes=[out])

    def stt(self, out, in0, scalar, in1, op0, op1, eng="dve"):
        o, x, y = out.ap, in0.ap, in1.ap
        rd = [in0, in1]
        if isinstance(scalar, Buf):
            s = scalar.ap
            rd.append(scalar)
        else:
            s = scalar
        return self.p.emit(eng, lambda e: e.scalar_tensor_tensor(out=o, in0=x, scalar=s, in1=y, op0=op0, op1=op1),
                           reads=rd, writes=[out])

    def ts(self, out, in0, s1, op0, s2=None, op1=None, eng="dve"):
        o, x = out.ap, in0.ap
        rd = [in0]
        a1 = s1
        if isinstance(s1, Buf):
            a1 = s1.ap
            rd.append(s1)
        a2 = s2
        if isinstance(s2, Buf):
            a2 = s2.ap
            rd.append(s2)
        if op1 is None:
            return self.p.emit(eng, lambda e: e.tensor_scalar(out=o, in0=x, scalar1=a1, scalar2=None, op0=op0),
                               reads=rd, writes=[out])
        return self.p.emit(eng, lambda e: e.tensor_scalar(out=o, in0=x, scalar1=a1, scalar2=a2, op0=op0, op1=op1),
                           reads=rd, writes=[out])

    def recip(self, out, in_, eng="dve"):
        o, i = out.ap, in_.ap
        return self.p.emit(eng, lambda e: e.reciprocal(out=o, in_=i), reads=[in_], writes=[out])

    def copy(self, out, in_, eng="dve"):
        o, i = out.ap, in_.ap
        return self.p.emit(eng, lambda e: e.tensor_copy(out=o, in_=i), reads=[in_], writes=[out])

    def memset(self, out, val, eng="dve"):
        o = out.ap
        return self.p.emit(eng, lambda e: e.memset(o, val), reads=[], writes=[out])


C_ONES, C_RMAT, C_GF1, C_GMIX, C_GF2, C_GQA, C_GKVA = 0, 128, 192, 208, 224, 240, 246
C_GQDA, C_GKDA, C_GQMN, C_GQMR, C_GKMN, C_GKMR, C_SUB0, C_SUB1, C_EPS, C_LAM = 250, 251, 252, 253, 254, 255, 256, 257, 258, 259
C_A2, C_AC, C_MM, CST_W = 264, 328, 1352, 2376

WIN_FM = [("q", 0, 16, 128), ("k", 2048, 16, 128), ("cq", 6144, 6, 128), ("ckv", 6912, 4, 128),
          ("kr", 7424, 1, 64), ("ga", 7488, 16, 128), ("gb", 9536, 16, 128)]
WIN_V0 = 4096


C_ONES, C_RMAT, C_GF1, C_GMIX, C_GF2, C_GQA, C_GKVA = 0, 128, 192, 208, 224, 240, 246
C_GQDA, C_GKDA, C_GQMN, C_GQMR, C_GKMN, C_GKMR, C_SUB0, C_SUB1, C_EPS, C_LAM = 250, 251, 252, 253, 254, 255, 256, 257, 258, 259
C_A2, C_AC, C_MM, CST_W = 264, 328, 1352, 2376
C_GQMRW = 263

WG = [("f1w1a", 24, 2048), ("f1w3a", 24, 2048), ("f1w1b", 24, 2048), ("f1w3b", 24, 2048), ("f1w2", 16, 5632), ("win", 80, 2048), ("winv", 8, 4096),
      ("wqb", 16, 1536), ("wkvbk", 16, 512), ("wkvbv", 8, 1024), ("wa", 16, 2048), ("wb", 16, 2048),
      ("wo", 16, 2048), ("f2w1", 48, 2048), ("f2w3", 48, 2048), ("f2w2", 16, 5632)]
WGD = {n: (t, e) for n, t, e in WG}
T_Q, T_K, T_CQ, T_CKV, T_KR, T_GA, T_GB = 0, 16, 32, 38, 42, 43, 59
GROWS = 2304
R_KDA, R_VDA, R_KMN, R_KMR, R_VM = 0, 512, 1024, 1536, 1792
SLOPES = [2.0 ** (-8.0 * (h + 1) / 8) for h in range(8)]
GKEEP = [1, 1, 1, 1, 2, 3, 6, 7]


def kpos(m):
    return 32 * (m // 32) + 4 * (m % 8) + (m % 32) // 8


class Kern(Builder):
    def dT(self, key):
        if key not in self.dram:
            self.dram[key] = T()
        return self.dram[key]

    def wtile(self, name, t, m):
        h = self.wg[name]
        ap = h[t * 128:(t + 1) * 128, :].rearrange("p (a b) -> p a b", b=m)
        return Buf(ap, [self.dT(("wg", name))])

    def dv(self, h, key, idx):
        return Buf(h[idx], [self.dT((h.name, key))])

    def gview(self, hnd, rank, row0, nrows, pattern, key, **sizes):
        tot = int(np.prod(hnd.shape))
        flat = hnd.reshape([tot])
        a = (rank * GROWS + row0) * 2048
        ap = flat[a:a + nrows * 2048].rearrange(pattern, **sizes)
        return Buf(ap, [self.dT((hnd.name, key))])

    def setup(self):
        nc = self.nc
        self.sbuf_init(204800)
        I = self.dram_in
        self.xT = I("xT", [D, NTOK])
        self.cst_d = I("cst", [128, CST_W])
        self.rope_d = I("rope", [64, 3, NTOK])
        tmp = self.dram_tmp
        self.wsrc, self.wown, self.wg = {}, {}, {}
        for n, nt, e in WG:
            self.wsrc[n] = I("ws_" + n, [nt // 8 * 128, e])
            self.wown[n] = tmp("wo_" + n, [nt // 8 * 128, e])
            self.wg[n] = tmp("wg_" + n, [nt * 128, e])
        self.out_d = self.dram_out("outT", [D, NTOK])
        self.x1_d = tmp("x1T", [KC, 128, NTOK], F32)
        self.qda_d = tmp("qda", [16, 128, NTOK])
        self.qmn_d = tmp("qmn", [16, 128, NTOK])
        self.qmr_d = tmp("qmr", [16, 64, NTOK])
        self.gT_d = tmp("gT", [32, 128, NTOK])
        self.yT_d = tmp("yT", [32, 128, NTOK])
        self.gin = [tmp("gin%d" % s, [GROWS, 2048]) for s in range(NSLOT)]
        self.gout = [tmp("gout%d" % s, [8 * GROWS, 2048]) for s in range(NSLOT)]
        sb = self.sb
        self.cst = sb(0, F32, [CST_W])
        self.ones_f = self.cst[:, C_ONES:C_ONES + 128]
        self.rmat = self.cst[0:64, C_RMAT:C_RMAT + 64]
        self.eps = self.cst[:, C_EPS:C_EPS + 1]
        self.mmk = self.cst[:, C_MM:C_MM + 1024]
        self.ones_b = sb(9728, BF16, [128])
        self.small = sb(10240, F32, [64])
        self.bpp = sb(10496, F32, [8, 8])
        self.acp = sb(C_AC * 4, BF16, [8, 128])
        self.strip = sb(C_AC * 4 + 2048, BF16, [1024])
        self.zeros_b = sb(10752, BF16, [TT])
        self.cos2 = sb(12544, F32, [TT], 64)
        self.sin2 = sb(14592, F32, [TT], 64)
        self.sinS = sb(182784, F32, [TT], 64)
        B0 = 16896
        self.B0 = B0
        self.xres = [sb(B0 + 2048 * k, F32, [TT]) for k in range(KC)]
        B1 = B0 + 32768
        self.hT = [sb(B1 + 1024 * k, BF16, [TT]) for k in range(KC)]
        B2 = B1 + 16384
        self.B2 = B2
        self.uT = [sb(B2 + 1024 * k, BF16, [TT]) for k in range(NF)]
        B3 = B2 + 45056
        self.B3 = B3
        self.wb = [sb(B3 + 4096 * k, BF16, [KC, 128]) for k in range(6)]
        B4 = B3 + 24576
        self.B4 = B4
        self.w2b = [sb(B4 + 11264 * k, BF16, [NF, 128]) for k in range(2)]
        B5 = B4 + 22528
        self.f32s = [sb(B5 + 2048 * k, F32, [TT]) for k in range(9)]
        self.sq = self.f32s[0:2]
        self.sq16 = [sb(B5 + 2048 * k, BF16, [TT]) for k in range(5)]
        self.rstd = self.f32s[2]
        self.sil = self.f32s[3:5]
        self.ob = [sb(B5 + 18432 + 1024 * k, BF16, [TT]) for k in range(6)]
        self.vb = [sb(B5 + 18432 + 2048 * k, BF16, [4, 256]) for k in range(3)]
        assert B5 + 24576 <= self.arena_bytes, B5 + 24576
        self.cnt = {}

    def rr(self, name, n):
        v = self.cnt.get(name, 0)
        self.cnt[name] = v + 1
        return v % n

    def wbv(self, kc, m):
        i = self.rr("wb", 10)
        off = self.B3 + 4096 * i if i < 6 else 186368 + 4096 * (i - 6)
        return self.sb(off, BF16, [kc, m])

    def consts(self):
        self.dma(self.cst, self.cst_d)
        self.memset(self.ones_b, 1.0)
        self.memset(self.zeros_b, 0.0)
        sm = self.small
        c = self.cst
        self.ts(sm[:, 0:1], c[:, C_GQDA:C_GQDA + 1], 128.0 ** -0.5, ALU.mult)
        self.ts(sm[:, 1:2], c[:, C_GQMN:C_GQMN + 1], 192.0 ** -0.5, ALU.mult)
        self.ts(sm[:, 2:3], c[:, C_GQMR:C_GQMR + 1], 192.0 ** -0.5, ALU.mult)
        self.ts(sm[:, 6:7], c[:, C_GQMRW:C_GQMRW + 1], 192.0 ** -0.5, ALU.mult)
        self.ts(sm[:, 3:4], c[:, C_SUB0:C_SUB0 + 1], 1.0 - LAMBDA_INIT, ALU.mult)
        self.ts(sm[:, 4:5], c[:, C_SUB1:C_SUB1 + 1], 1.0 - LAMBDA_INIT, ALU.mult)
        self.tt(sm[:, 8:9], c[:, C_LAM:C_LAM + 1], c[:, C_LAM + 1:C_LAM + 2], ALU.mult)
        self.tt(sm[:, 9:10], c[:, C_LAM + 2:C_LAM + 3], c[:, C_LAM + 3:C_LAM + 4], ALU.mult)
        ps = self.ps(7, 2)
        self.mm(ps, self.ones_f, sm[:, 8:10], True, True)
        self.act(sm[:, 10:12], ps, AF.Exp)
        self.tt(sm[:, 12:13], sm[:, 11:12], sm[:, 10:11], ALU.subtract)
        self.ts(sm[:, 5:6], sm[:, 12:13], -LAMBDA_INIT, ALU.add)
        for h in range(8):
            self.ts(self.bpp[:, h, :], c[:, C_A2:C_A2 + 8], SLOPES[h], ALU.mult)

    def convert(self, names):
        for n in names:
            nt, e = WGD[n]
            for t in range(nt // 8):
                sl = slice(t * 128, (t + 1) * 128)
                src = Buf(self.wsrc[n].ap[sl, :], self.wsrc[n].ts)
                dst = Buf(self.wown[n][sl, :], [self.dT(("wown", n))])
                self.dma(dst, src, eng="pool")

    def gather_w(self, names, qos="P3"):
        for n in names:
            src = Buf(self.wown[n].ap(), [self.dT(("wown", n))])
            dst = Buf(self.wg[n].ap(), [self.dT(("wg", n))])
            self.allgather(dst, src, None if n in ("f1w1a", "f1w3a") else qos)

    def allgather(self, dst, src, qos="P3"):
        o, i = dst.ap.opt(), src.ap.opt()
        self.p.emit("pool", lambda e: e.collective_compute("AllGather", ALU.bypass, replica_groups=[list(range(NCORES))],
                                                           ins=[i], outs=[o], dma_qos=qos),
                    reads=[src], writes=[dst], kind="cc")

    def rmsnorm(self, gcol, bank):
        ps = self.ps(bank)
        for kc in range(KC):
            sq = self.sq16[kc % 2]
            self.act(sq, self.xres[kc], AF.Square)
            self.mm(ps, self.ones_b, sq, kc == 0, kc == KC - 1)
        self.act(self.rstd, ps, AF.Sqrt, bias=self.eps, scale=1.0 / D)
        self.recip(self.rstd, self.rstd)
        for kc in range(KC):
            self.stt(self.hT[kc], self.xres[kc], self.cst[:, gcol + kc:gcol + kc + 1], self.rstd, ALU.mult, ALU.mult)

    def ffn(self, k):
        for f in range(NF):
            b1, b3 = self.wbv(KC, 128), self.wbv(KC, 128)
            if k == "f1":
                sfx, fi = ("a", f) if f < 24 else ("b", f - 24)
            else:
                sfx, fi = "", f
            self.dma(b1, self.wtile(k + "w1" + sfx, fi, 128))
            self.dma(b3, self.wtile(k + "w3" + sfx, fi, 128))
            pA, pB = self.ps((f % 2) * 2), self.ps((f % 2) * 2 + 1)
            for kc in range(KC):
                self.mm(pA, b1[:, kc, :], self.hT[kc], kc == 0, kc == KC - 1)
            for kc in range(KC):
                self.mm(pB, b3[:, kc, :], self.hT[kc], kc == 0, kc == KC - 1)
            sil = self.sil[f % 2]
            self.act(sil, pA, AF.Silu)
            self.tt(self.uT[f], sil, pB, ALU.mult)
        for dc in range(KC):
            wb2 = self.w2b[dc % 2]
            self.dma(wb2, self.wtile(k + "w2", dc, 128))
            pO = self.ps(4 + dc % 2)
            for f in range(NF):
                self.mm(pO, wb2[:, f, :], self.uT[f], f == 0, f == NF - 1)
            self.stt(self.xres[dc], pO, 0.5, self.xres[dc], ALU.mult, ALU.add)

    def pipe(self, n, s1, s2):
        for i in range(n):
            s1(i)
            if i > 0:
                s2(i - 1)
        s2(n - 1)

    def rstd_from(self, ps2, ndim, parts=128):
        r = self.f32s[5 + self.rr("r", 2)]
        self.act(r[0:parts], ps2[0:parts], AF.Sqrt, bias=self.eps[0:parts], scale=1.0 / ndim)
        self.recip(r[0:parts], r[0:parts])
        return r

    def proj_qk_da(self, s):
        sl = slice(s * TT, (s + 1) * TT)
        kview = self.gview(self.gin[s], 0, R_KDA, 512, "(u d t) -> u d t", "kda", u=16, d=128)
        st = {}

        def s1(i):
            isq, u = i // 16, i % 16
            w = self.wbv(KC, 128)
            self.dma(w, self.wtile("win", (T_Q if isq == 0 else T_K) + u, 128))
            ps = self.ps(self.rr("pA", 4))
            for kc in range(KC):
                self.mm(ps, w[:, kc, :], self.hT[kc], kc == 0, kc == KC - 1)
            sq = self.sq16[self.rr("sq", 5)]
            self.act(sq, ps, AF.Square)
            st[i] = (ps, sq)

        def s2(i):
            isq, u = i // 16, i % 16
            ps, sq = st.pop(i)
            ps2 = self.ps(4 + self.rr("pB", 2))
            self.mm(ps2, self.ones_b, sq, True, True)
            r = self.rstd_from(ps2, 128)
            ob = self.ob[self.rr("ob", 6)]
            g = self.small[:, 0:1] if isq == 0 else self.cst[:, C_GKDA:C_GKDA + 1]
            self.stt(ob, ps, g, r, ALU.mult, ALU.mult)
            if isq == 0:
                self.dma(self.dv(self.qda_d, ("q", u, s), (u, slice(None), sl)), ob, eng="pool")
            else:
                self.dma(kview.v(kview.ap[u]), ob, eng="pool")

        self.pipe(32, s1, s2)

    def proj_v_da(self, s):
        vview = self.gview(self.gin[s], 0, R_VDA, 512, "(h t jl e) -> h t jl e", "vda", h=8, t=128, jl=4)
        for g in range(8):
            w = self.sb(self.B4 + 11264 * (g % 2), BF16, [KC, 256])
            self.dma(w, self.wtile("winv", g, 256))
            vb = self.vb[self.rr("vb", 3)]
            for tb in range(4):
                ps = self.ps(self.rr("pA", 4), 256)
                for kc in range(KC):
                    self.mm(ps, self.hT[kc][:, tb * 128:(tb + 1) * 128], w[:, kc, :], kc == 0, kc == KC - 1)
                self.act(vb[:, tb, :], ps, AF.Copy)
            self.dma(vview.v(vview.ap[g]), vb, eng="pool")

    def proj_gates(self, s):
        sl = slice(s * TT, (s + 1) * TT)
        for u in range(32):
            w = self.wbv(KC, 128)
            self.dma(w, self.wtile("win", T_GA + u, 128))
            ps = self.ps(self.rr("pA", 4))
            for kc in range(KC):
                self.mm(ps, w[:, kc, :], self.hT[kc], kc == 0, kc == KC - 1)
            ob = self.ob[self.rr("ob", 6)]
            self.act(ob, ps, AF.Sigmoid)
            self.dma(self.dv(self.gT_d, ("g", u, s), (u, slice(None), sl)), ob, eng="pool")

    def latent(self, tbase, nch, gcol, f32buf, nbuf):
        ps2 = self.ps(6 + self.rr("pC", 2))
        for i in range(nch):
            w = self.wbv(KC, 128)
            self.dma(w, self.wtile("win", tbase + i, 128))
            ps = self.ps(self.rr("pA", 4))
            for kc in range(KC):
                self.mm(ps, w[:, kc, :], self.hT[kc], kc == 0, kc == KC - 1)
            self.act(f32buf[i], ps, AF.Copy)
            sq = self.sq16[self.rr("sq", 5)]
            self.act(sq, ps, AF.Square)
            self.mm(ps2, self.ones_b, sq, i == 0, i == nch - 1)
        r = self.rstd_from(ps2, nch * 128)
        for i in range(nch):
            self.stt(nbuf[i], f32buf[i], self.cst[:, gcol + i:gcol + i + 1], r, ALU.mult, ALU.mult)

    def rope(self, xr, out_bf):
        psr = self.ps(4 + self.rr("pB", 2), 512, 64)
        self.mm(psr, self.rmat, xr, True, True)
        t1, t2 = self.f32s[7][0:64], self.f32s[8][0:64]
        self.tt(t1, xr, self.cos2, ALU.mult)
        self.tt(t2, psr, self.sin2, ALU.mult)
        self.tt(out_bf, t1, t2, ALU.add)

    def proj_mla(self, s):
        sl = slice(s * TT, (s + 1) * TT)
        U = self.B2
        cqf = [self.sb(U + 2048 * i, F32, [TT]) for i in range(6)]
        cqn = [self.sb(U + 12288 + 1024 * i, BF16, [TT]) for i in range(6)]
        ckf = [self.sb(U + 18432 + 2048 * i, F32, [TT]) for i in range(4)]
        ckn = [self.sb(U + 26624 + 1024 * i, BF16, [TT]) for i in range(4)]
        krb = self.sb(U + 30720, F32, [TT], 64)
        sqkr = self.sb(U + 32768, BF16, [TT], 64)
        xrq = self.sb(U + 34816, F32, [TT], 64)
        krg = self.sb(U + 36864, F32, [TT], 64)
        self.dma(self.cos2, Buf(self.rope_d.ap[:, 0, sl], self.rope_d.ts))
        self.dma(self.sin2, Buf(self.rope_d.ap[:, 1, sl], self.rope_d.ts))
        self.dma(self.sinS, Buf(self.rope_d.ap[:, 2, sl], self.rope_d.ts))
        self.latent(T_CQ, 6, C_GQA, cqf, cqn)
        st = {}

        cg, sg = self.f32s[7][0:64], self.f32s[8][0:64]
        self.ts(cg, self.cos2, self.small[0:64, 2:3], ALU.mult)
        self.ts(sg, self.sinS, self.small[0:64, 6:7], ALU.mult)

        def q1(h):
            w = self.wbv(6, 256)
            self.dma(w, self.wtile("wqb", h, 256))
            psn = self.ps(self.rr("pA", 4))
            psr = self.ps(self.rr("pA", 4), 512, 64)
            psw = self.ps(4 + self.rr("pB", 2), 512, 64)
            for kc in range(6):
                self.mm(psn, w[:, kc, 0:128], cqn[kc], kc == 0, kc == 5)
            for kc in range(6):
                self.mm(psr, w[:, kc, 128:192], cqn[kc], kc == 0, kc == 5)
            for kc in range(6):
                self.mm(psw, w[:, kc, 192:256], cqn[kc], kc == 0, kc == 5)
            sqn = self.sq16[self.rr("sq", 5)]
            sqr = self.sq16[self.rr("sq", 5)]
            self.act(sqn, psn, AF.Square)
            self.act(sqr[0:64], psr, AF.Square)
            xn, xr_, xw_ = cqf[3 * (h % 2)], cqf[3 * (h % 2) + 1][0:64], cqf[3 * (h % 2) + 2][0:64]
            self.act(xn, psn, AF.Copy)
            self.act(xr_, psr, AF.Copy)
            self.act(xw_, psw, AF.Copy)
            st[h] = (xn, xr_, xw_, sqn, sqr)

        def q2(h):
            psn, psr, psw, sqn, sqr = st.pop(h)
            ps2 = self.ps(6 + self.rr("pC", 2))
            self.mm(ps2, self.ones_b, sqn, True, False)
            self.mm(ps2, self.ones_b[0:64, :], sqr[0:64], False, True)
            r = self.rstd_from(ps2, 192)
            ob = self.ob[self.rr("ob", 6)]
            self.stt(ob, psn, self.small[:, 1:2], r, ALU.mult, ALU.mult)
            self.dma(self.dv(self.qmn_d, ("q", h, s), (h, slice(None), sl)), ob, eng="pool")
            self.tt(xrq, psr, cg, ALU.mult)
            self.tt(krg, psw, sg, ALU.mult)
            self.tt(xrq, xrq, krg, ALU.add)
            ob2 = self.ob[self.rr("ob", 6)]
            self.tt(ob2[0:64], xrq, r[0:64], ALU.mult)
            self.dma(self.dv(self.qmr_d, ("q", h, s), (h, slice(None), sl)), ob2[0:64], eng="pool")

        self.pipe(16, q1, q2)
        self.latent(T_CKV, 4, C_GKVA, ckf, ckn)
        w = self.wbv(KC, 128)
        self.dma(w, self.wtile("win", T_KR, 128))
        pskr = self.ps(self.rr("pA", 4), 512, 64)
        for kc in range(KC):
            self.mm(pskr, w[:, kc, 0:64], self.hT[kc], kc == 0, kc == KC - 1)
        self.act(sqkr, pskr, AF.Square)
        self.ts(krg, pskr, self.cst[0:64, C_GKMR:C_GKMR + 1], ALU.mult)
        self.rope(krg, krb)
        knview = self.gview(self.gin[s], 0, R_KMN, 512, "(u d t) -> u d t", "kmn", u=16, d=128)
        krview = self.gview(self.gin[s], 0, R_KMR, 256, "(u d t) -> u d t", "kmr", u=16, d=64)

        def k1(h):
            w = self.wbv(4, 128)
            self.dma(w, self.wtile("wkvbk", h, 128))
            psn = self.ps(self.rr("pA", 4))
            for kc in range(4):
                self.mm(psn, w[:, kc, :], ckn[kc], kc == 0, kc == 3)
            sqn = self.sq16[self.rr("sq", 5)]
            self.act(sqn, psn, AF.Square)
            st[h] = (psn, sqn)

        def k2(h):
            psn, sqn = st.pop(h)
            ps2 = self.ps(6 + self.rr("pC", 2))
            self.mm(ps2, self.ones_b, sqn, True, False)
            self.mm(ps2, self.ones_b[0:64, :], sqkr, False, True)
            r = self.rstd_from(ps2, 192)
            ob = self.ob[self.rr("ob", 6)]
            self.stt(ob, psn, self.cst[:, C_GKMN:C_GKMN + 1], r, ALU.mult, ALU.mult)
            self.dma(knview.v(knview.ap[h]), ob, eng="pool")
            ob2 = self.ob[self.rr("ob", 6)]
            self.tt(ob2[0:64], krb, r[0:64], ALU.mult)
            self.dma(krview.v(krview.ap[h]), ob2[0:64], eng="pool")

        self.pipe(16, k1, k2)
        vview = self.gview(self.gin[s], 0, R_VM, 512, "(h t jl e) -> h t jl e", "vm", h=16, t=128, jl=4)
        for g in range(8):
            w = self.wbv(4, 256)
            self.dma(w, self.wtile("wkvbv", g, 256))
            vb = self.vb[self.rr("vb", 3)]
            for tb in range(4):
                ps = self.ps(self.rr("pA", 4), 256)
                for kc in range(4):
                    self.mm(ps, ckn[kc][:, tb * 128:(tb + 1) * 128], w[:, kc, :], kc == 0, kc == 3)
                self.act(vb[:, tb, :], ps, AF.Copy)
            for hl in range(2):
                self.dma(vview.v(vview.ap[2 * g + hl]), vb[:, :, hl * 128:(hl + 1) * 128], eng="pool")

    def phase1(self, s):
        sl = slice(s * TT, (s + 1) * TT)
        for kc in range(KC):
            self.dma(self.xres[kc], self.xT.v(self.xT.ap[kc * 128:(kc + 1) * 128, sl]))
        self.rmsnorm(C_GF1, 6 + s % 2)
        self.ffn("f1")
        for kc in range(KC):
            self.dma(self.dv(self.x1_d, ("x1", kc, s), (kc, slice(None), sl)), self.xres[kc], eng="pool")
        self.rmsnorm(C_GMIX, 6 + (s + 1) % 2)
        self.proj_qk_da(s)
        self.proj_v_da(s)
        self.proj_gates(s)
        self.proj_mla(s)
        keys = ["kda", "vda", "kmn", "kmr", "vm"]
        src = Buf(self.gin[s].ap(), [self.dT((self.gin[s].name, k)) for k in keys])
        dst = Buf(self.gout[s].ap(), [self.dT((self.gout[s].name, "all"))])
        self.allgather(dst, src)

    def load_kv(self, dst, s_pair, row0, nrows, pattern, idx, sizes, eng="sp"):
        for half in range(2):
            go = self.gout[s_pair + half]
            for r in range(8):
                v = self.gview(go, r, row0, nrows, pattern, "all", **sizes)
                src = v.v(v.ap[idx])
                p0 = 32 * half + 4 * r
                self.dma(dst[:, p0:p0 + 4, :], src, eng=eng)

    def attention(self, b):
        A = self.B0
        sb = self.sb
        vbuf = [sb(A + 32768 * i, BF16, [64, 256]) for i in range(2)]
        kbuf = [sb(A + 65536 + 16384 * i, BF16, [64, 128]) for i in range(2)]
        krbuf = [sb(A + 98304 + 16384 * i, BF16, [64, 128], 64) for i in range(2)]
        Q = A + 131072
        qn = [sb(Q + 2048 * i, BF16, [1024]) for i in range(2)]
        qr = [sb(Q + 4096 + 2048 * i, BF16, [1024], 64) for i in range(2)]
        pt = [sb(Q + 8192 + 1024 * i, BF16, [TT]) for i in range(4)]
        tmpb = [sb(Q + 34816 + 512 * i, F32, [128]) for i in range(3)]
        o1 = [[sb(Q + 13312 + 4096 * q + 2048 * e, F32, [TT]) for e in range(2)] for q in range(2)]
        rinv = sb(Q + 21504, F32, [TT])
        tf = sb(Q + 23552, F32, [TT])
        sqb = [sb(Q + 25600 + 2048 * i, BF16, [TT]) for i in range(2)]
        rsub = sb(Q + 29696, F32, [TT])
        yb = [sb(Q + 31744 + 1024 * i, BF16, [TT]) for i in range(3)]
        tmpf = [sb(Q + 36352 + 2048 * i, F32, [TT]) for i in range(3)]
        assert Q + 42496 <= self.arena_bytes
        sp = 2 * b
        col0 = b * 1024
        neglam = self.small[:, 5:6]

        def kloop(qh, kb, krb_, qnb, qrb, vb_, nec, banks, da=None, G=7):
            psO = [self.ps(banks[e]) for e in range(nec)]
            psS_ = self.ps(banks[2])
            jA, jB = 4 * qh, 4 * qh + 3
            g0 = max(0, jA - G)
            steps = []
            for g in range(g0, jB + 1):
                jlo, jhi = max(g, jA), min(g + G, jB)
                if jlo <= jhi:
                    for mp in range(8):
                        steps.append((8 * g + mp, g, mp, jlo, jhi))
            first_g = {j: max(g0, j - G) for j in range(jA, jB + 1)}
            stA = {}

            def qk(i):
                m, g, mp, jlo, jhi = steps[i]
                n = 128 * (jhi - jlo + 1)
                q0 = 128 * jlo
                ps = self.ps(self.rr("pS", 3), n)
                km = kpos(m)
                if krb_ is None:
                    self.mm(ps, kb[:, km, :], qnb[:, q0:q0 + n], True, True)
                else:
                    self.mm(ps, kb[:, km, :], qnb[:, q0:q0 + n], True, False)
                    self.mm(ps, krb_[:, km, :], qrb[:, q0:q0 + n], False, True)
                p = pt[self.rr("pt", 4)]
                if da is not None:
                    h, slope = da
                    t = tmpf[self.rr("tmpf", 3)]
                    if jlo == g:
                        self.stt(t[:, 0:128], self.acp[:, mp, :], slope, ps[:, 0:128], ALU.mult, ALU.add)
                        if n > 128:
                            self.stt(t[:, 128:n], self.strip[:, 128:n], slope, ps[:, 128:n], ALU.mult, ALU.add)
                    else:
                        off = 128 * (jlo - g)
                        self.stt(t[:, 0:n], self.strip[:, off:off + n], slope, ps[:, 0:n], ALU.mult, ALU.add)
                    self.act(p[:, 0:n], t[:, 0:n], AF.Exp, bias=self.bpp[:, h, mp:mp + 1])
                else:
                    a = 0
                    if jlo == g:
                        tb_ = tmpb[self.rr("tmpb", 3)]
                        self.tt(tb_, ps[:, 0:128], self.mmk[:, mp * 128:(mp + 1) * 128], ALU.add)
                        self.act(p[:, 0:128], tb_, AF.Exp)
                        a = 128
                    if n > a:
                        self.act(p[:, a:n], ps[:, a:n], AF.Exp)
                stA[i] = (p, n, 128 * (jlo - jA), km)

            def pv(i):
                m, g, mp, jlo, jhi = steps[i]
                p, n, c0, km = stA.pop(i)
                for e in range(nec):
                    self.mm(psO[e][:, c0:c0 + n], vb_[:, km, e * 128:(e + 1) * 128], p[:, 0:n], False, False)
                self.mm(psS_[:, c0:c0 + n], self.ones_b, p[:, 0:n], False, False)

            for e in range(nec):
                self.mm(psO[e], self.ones_b, self.zeros_b, True, False)
            self.mm(psS_, self.ones_b, self.zeros_b, True, False)
            LA = 2
            ns = len(steps)
            for i in range(ns + LA):
                if i < ns:
                    qk(i)
                if i >= LA:
                    pv(i - LA)
            return psO, psS_

        for h in range(8):
            vb_ = vbuf[self.rr("vbuf", 2)]
            self.load_kv(vb_, sp, R_VDA, 512, "(h t jl e) -> h t jl e", h, dict(h=8, t=128, jl=4))
            for mp_ in range(2):
                u = 2 * h + mp_
                kb = kbuf[self.rr("kbuf", 2)]
                self.load_kv(kb, sp, R_KDA, 512, "(u d t) -> u d t", u, dict(u=16, d=128))
                qnb = qn[self.rr("qn", 2)]
                self.dma(qnb, Buf(self.qda_d[u, :, col0:col0 + 1024], [self.dT((self.qda_d.name, ("q", u, 2 * b))),
                                                                       self.dT((self.qda_d.name, ("q", u, 2 * b + 1)))]))
                for qh in range(2):
                    par = self.rr("oset", 2)
                    banks = (3, 4, 5) if par == 0 else (6, 7, 5)
                    psO, psS_ = kloop(qh, kb, None, qnb, None, vb_, 2, banks, da=(h, SLOPES[h]), G=GKEEP[h])
                    self.recip(rinv, psS_)
                    if mp_ == 0:
                        for e in range(2):
                            self.tt(o1[qh][e], psO[e], rinv, ALU.mult)
                    else:
                        for e in range(2):
                            self.tt(tf, psO[e], rinv, ALU.mult)
                            self.stt(o1[qh][e], tf, neglam, o1[qh][e], ALU.mult, ALU.add)
                        ps2 = self.ps(self.rr("pS", 3))
                        for e in range(2):
                            self.act(sqb[e], o1[qh][e], AF.Square)
                            self.mm(ps2, self.ones_b, sqb[e], e == 0, e == 1)
                        self.act(rsub, ps2, AF.Sqrt, bias=self.eps, scale=1.0 / 256)
                        self.recip(rsub, rsub)
                        for e in range(2):
                            y = yb[self.rr("yb", 3)]
                            self.stt(y, o1[qh][e], self.small[:, 3 + e:4 + e], rsub, ALU.mult, ALU.mult)
                            c = col0 + qh * TT
                            self.dma(self.dv(self.yT_d, ("y", 2 * h + e, 2 * b + qh), (2 * h + e, slice(None), slice(c, c + TT))), y, eng="pool")
        for h in range(16):
            vb_ = vbuf[self.rr("vbuf", 2)]
            vbm = vb_.v(vb_.ap[:, :, 0:128])
            self.load_kv(vbm, sp, R_VM, 512, "(h t jl e) -> h t jl e", h, dict(h=16, t=128, jl=4))
            kb = kbuf[self.rr("kbuf", 2)]
            self.load_kv(kb, sp, R_KMN, 512, "(u d t) -> u d t", h, dict(u=16, d=128))
            krb_ = krbuf[self.rr("krbuf", 2)]
            self.load_kv(krb_, sp, R_KMR, 256, "(u d t) -> u d t", h, dict(u=16, d=64))
            qnb = qn[self.rr("qn", 2)]
            qrb = qr[self.rr("qr", 2)]
            ts_ = [self.dT((self.qmn_d.name, ("q", h, 2 * b))), self.dT((self.qmn_d.name, ("q", h, 2 * b + 1)))]
            self.dma(qnb, Buf(self.qmn_d[h, :, col0:col0 + 1024], ts_))
            ts_ = [self.dT((self.qmr_d.name, ("q", h, 2 * b))), self.dT((self.qmr_d.name, ("q", h, 2 * b + 1)))]
            self.dma(qrb, Buf(self.qmr_d[h, :, col0:col0 + 1024], ts_))

            for qh in range(2):
                par = self.rr("oset", 2)
                banks = (3, 4, 5) if par == 0 else (6, 7, 5)
                psO, psS_ = kloop(qh, kb, krb_, qnb, qrb, vbm, 1, banks)
                self.recip(rinv, psS_)
                y = yb[self.rr("yb", 3)]
                self.tt(y, psO[0], rinv, ALU.mult)
                c = col0 + qh * TT
                self.dma(self.dv(self.yT_d, ("y", 16 + h, 2 * b + qh), (16 + h, slice(None), slice(c, c + TT))), y, eng="pool")

    def phase3(self, s):
        sl = slice(s * TT, (s + 1) * TT)
        U = self.B2
        ya = [self.sb(U + 1024 * i, BF16, [TT]) for i in range(16)]
        yb_ = [self.sb(U + 16384 + 1024 * i, BF16, [TT]) for i in range(16)]
        for i in range(16):
            self.dma(ya[i], self.dv(self.yT_d, ("y", i, s), (i, slice(None), sl)))
            self.dma(yb_[i], self.dv(self.yT_d, ("y", 16 + i, s), (16 + i, slice(None), sl)))
        for oc in range(16):
            wa_, wb_ = self.wbv(KC, 128), self.wbv(KC, 128)
            self.dma(wa_, self.wtile("wa", oc, 128))
            self.dma(wb_, self.wtile("wb", oc, 128))
            ga, gb = self.ob[self.rr("ob", 6)], self.ob[self.rr("ob", 6)]
            self.dma(ga, self.dv(self.gT_d, ("g", oc, s), (oc, slice(None), sl)))
            self.dma(gb, self.dv(self.gT_d, ("g", 16 + oc, s), (16 + oc, slice(None), sl)))
            pa, pb = self.ps(self.rr("pA", 4)), self.ps(self.rr("pA", 4))
            for kc in range(KC):
                self.mm(pa, wa_[:, kc, :], ya[kc], kc == 0, kc == KC - 1)
            for kc in range(KC):
                self.mm(pb, wb_[:, kc, :], yb_[kc], kc == 0, kc == KC - 1)
            m1 = self.f32s[self.rr("sq", 5)]
            self.tt(m1, pa, ga, ALU.mult)
            m2 = self.f32s[self.rr("sq", 5)]
            self.tt(m2, pb, gb, ALU.mult)
            self.tt(self.hT[oc], m1, m2, ALU.add)
        for kc in range(KC):
            self.dma(self.xres[kc], self.dv(self.x1_d, ("x1", kc, s), (kc, slice(None), sl)))
        for oc in range(16):
            w = self.wbv(KC, 128)
            self.dma(w, self.wtile("wo", oc, 128))
            ps = self.ps(4 + self.rr("pB", 2))
            for kc in range(KC):
                self.mm(ps, w[:, kc, :], self.hT[kc], kc == 0, kc == KC - 1)
            self.tt(self.xres[oc], ps, self.xres[oc], ALU.add)
        self.rmsnorm(C_GF2, 6 + s % 2)
        self.ffn("f2")
        for kc in range(KC):
            self.dma(self.out_d.v(self.out_d.ap[kc * 128:(kc + 1) * 128, sl]), self.xres[kc], eng="pool")

    def build(self):
        self.setup()
        self.consts()
        first = ["f1w1a", "f1w3a", "f1w1b", "f1w3b", "f1w2", "win", "winv", "wqb", "wkvbk", "wkvbv"]
        rest = ["wa", "wb", "wo", "f2w1", "f2w3", "f2w2"]
        for n in first:
            self.convert([n])
            self.gather_w([n])
        self.convert(rest)
        for s in range(NSLOT):
            self.phase1(s)
            if s == 0:
                self.gather_w(rest)
        for b in range(2):
            self.attention(b)
        for s in range(NSLOT):
            self.phase3(s)
        if self.stage == 50:
            for hnd, dt in [(self.x1_d, F32), (self.qda_d, BF16), (self.qmn_d, BF16), (self.qmr_d, BF16), (self.gT_d, BF16),
                            (self.yT_d, BF16), (self.gin[0], BF16), (self.gin[3], BF16), (self.gout[1], BF16)]:
                o = self.dram_out("dbg_" + hnd.name, list(hnd.shape), dt)
                ts_ = [t for k, t in self.dram.items() if isinstance(k, tuple) and k[0] == hnd.name]
                self.dma(o, Buf(hnd.ap(), ts_))
        return self.finish()

    def finish(self):
        nc = self.nc
        sems = self.p.finalize()
        semh = {}
        for i, s in enumerate(sems):
            semh[s] = self.es.enter_context(nc.semaphore("s%d" % i))
        finals = {}
        for e in ENGS:
            for ins in self.p.streams[e]:
                if ins.kind != "c":
                    finals[ins.sem] = max(finals.get(ins.sem, 0), ins.val)
        fw = sorted(finals.items())
        p = self.p
        with nc.Block() as block:
            @block.tensor
            def _(e):
                p.replay("pe", e, semh)

            @block.scalar
            def _(e):
                p.replay("act", e, semh)

            @block.vector
            def _(e):
                p.replay("dve", e, semh)

            @block.gpsimd
            def _(e):
                p.replay("pool", e, semh)

            @block.sync
            def _(e):
                p.replay("sp", e, semh, final_waits=fw)
        self.es.close()
        return nc


def _tok_index(c):
    idx = np.empty((8, 128), np.int64)
    for j in range(8):
        idx[j, :] = (8 * j + c) * 128 + np.arange(128)
    return idx.reshape(1024)


def _consts(c, inp):
    cst = np.zeros((128, CST_W), np.float32)
    cst[:, C_ONES:C_ONES + 128] = 1.0
    for i in range(32):
        cst[i + 32, C_RMAT + i] = -1.0
        cst[i, C_RMAT + 32 + i] = 1.0

    def fm(v, n):
        return np.asarray(v, np.float32).reshape(n, 128).T
    cst[:, C_GF1:C_GF1 + 16] = fm(inp["ffn1_norm_g"][0], 16)
    cst[:, C_GMIX:C_GMIX + 16] = fm(inp["mix_norm_g"][0], 16)
    cst[:, C_GF2:C_GF2 + 16] = fm(inp["ffn2_norm_g"][0], 16)
    cst[:, C_GQA:C_GQA + 6] = fm(inp["mla_q_a_norm_g"][0], 6)
    cst[:, C_GKVA:C_GKVA + 4] = fm(inp["mla_kv_a_norm_g"][0], 4)
    cst[:, C_GQDA] = inp["da_q_norm_g"][0]
    cst[:, C_GKDA] = inp["da_k_norm_g"][0]
    cst[:, C_GQMN] = inp["mla_q_norm_g"][0][:128]
    cst[:64, C_GQMR] = inp["mla_q_norm_g"][0][128:]
    cst[:32, C_GQMRW] = inp["mla_q_norm_g"][0][160:]
    cst[32:64, C_GQMRW] = inp["mla_q_norm_g"][0][128:160]
    cst[:, C_GKMN] = inp["mla_k_norm_g"][0][:128]
    cst[:64, C_GKMR] = inp["mla_k_norm_g"][0][128:]
    cst[:, C_SUB0] = inp["da_subln_g"][0][:128]
    cst[:, C_SUB1] = inp["da_subln_g"][0][128:]
    cst[:, C_EPS] = EPS
    for i, n in enumerate(["da_lambda_q1", "da_lambda_k1", "da_lambda_q2", "da_lambda_k2"]):
        cst[:, C_LAM + i] = inp[n][0]
    p = np.arange(128, dtype=np.float64)
    for m in range(8):
        cst[:, C_A2 + m] = p - 64 - 128 * (c - m)
    f = np.arange(128, dtype=np.float64)
    chunk_ok = (p[:, None] // 64) <= (f[None, :] // 64)
    acp = np.zeros((128, 8, 128), np.float32)
    for m in range(8):
        if m < c:
            mk = np.zeros((128, 128))
        elif m == c:
            d2 = 2.0 * (f[None, :] - p[:, None])
            acp[:, m, :] = np.where(chunk_ok, np.minimum(d2, 0.0), NEG)
            mk = np.where(chunk_ok, 0.0, -30000.0)
        else:
            acp[:, m, :] = NEG
            mk = np.full((128, 128), -30000.0)
        cst[:, C_MM + m * 128:C_MM + (m + 1) * 128] = mk
    strip = np.broadcast_to(-1024.0 * (np.arange(1024) // 128)[None, :], (128, 1024)).astype(np.float32)
    packed = np.ascontiguousarray(np.concatenate([acp.reshape(128, 1024), strip], axis=1).astype(ml_dtypes.bfloat16))
    cst[:, C_AC:C_AC + 1024] = packed.view(np.uint16).reshape(128, 2048).view(np.uint32).view(np.float32)
    return cst


def _rope_tab(c):
    pos = np.concatenate([_tok_index(c), _tok_index(c)]).astype(np.float32)
    inv = (10000.0 ** (-np.arange(0, 64, 2, dtype=np.float32) / 64)).astype(np.float32)
    ang = pos[None, :] * inv[:, None]
    cs, sn = np.cos(ang).astype(np.float32), np.sin(ang).astype(np.float32)
    tab = np.empty((64, 3, NTOK), np.float32)
    tab[:32, 0], tab[32:, 0] = cs, cs
    tab[:32, 1], tab[32:, 1] = sn, sn
    tab[:32, 2], tab[32:, 2] = -sn, sn
    return tab


def _fm_tiles(W, c0, ntile, M, mt, stride=None):
    K = W.shape[0]
    kcn = K // 128
    stride = M if stride is None else stride
    out = np.zeros((ntile, 128, kcn, mt), np.float32)
    for i in range(ntile):
        blk = W[:, c0 + i * stride:c0 + i * stride + M].reshape(kcn, 128, M)
        out[i, :, :, :M] = blk.transpose(1, 0, 2)
    return out.reshape(ntile, 128, kcn * mt)


def _weight_groups(inp):
    g = lambda n: np.asarray(inp[n], np.float32)[0]
    W = {}
    for k, n in (("f1", "ffn1"), ("f2", "ffn2")):
        W[k + "w1"] = _fm_tiles(g(n + "_w1"), 0, 44, 128, 128)
        W[k + "w3"] = _fm_tiles(g(n + "_w3"), 0, 44, 128, 128)
        W[k + "w2"] = _fm_tiles(g(n + "_w2"), 0, 16, 128, 128)
    for nm in ("f1w1", "f1w3"):
        W[nm + "a"], W[nm + "b"] = W[nm][:24], W[nm][24:]
    win = g("w_in")
    parts = [_fm_tiles(win, 0, 16, 128, 128), _fm_tiles(win, 2048, 16, 128, 128), _fm_tiles(win, 6144, 6, 128, 128),
             _fm_tiles(win, 6912, 4, 128, 128), _fm_tiles(win, 7424, 1, 64, 128), _fm_tiles(win, 7488, 32, 128, 128)]
    W["win"] = np.concatenate(parts, 0)
    W["winv"] = _fm_tiles(win, 4096, 8, 256, 256)
    wq = g("mla_w_qb").reshape(768, 16, 192)
    wq = np.concatenate([wq, wq[:, :, 160:192], wq[:, :, 128:160]], axis=2).reshape(768, 16 * 256)
    W["wqb"] = _fm_tiles(wq, 0, 16, 256, 256)
    kvb = g("mla_w_kvb")
    W["wkvbk"] = _fm_tiles(kvb, 0, 16, 128, 128, stride=256)
    v = np.zeros((8, 128, 4, 256), np.float32)
    for gi in range(8):
        for hl in range(2):
            h = 2 * gi + hl
            v[gi, :, :, hl * 128:(hl + 1) * 128] = kvb[:, h * 256 + 128:h * 256 + 256].reshape(4, 128, 128).transpose(1, 0, 2)
    W["wkvbv"] = v.reshape(8, 128, 1024)
    W["wa"] = _fm_tiles(g("w_branch_a"), 0, 16, 128, 128)
    W["wb"] = _fm_tiles(g("w_branch_b"), 0, 16, 128, 128)
    W["wo"] = _fm_tiles(g("w_out"), 0, 16, 128, 128)
    out = {}
    for n, nt, e in WG:
        a = W[n]
        if a.shape[0] < nt:
            a = np.concatenate([a, np.zeros((nt - a.shape[0], 128, e), np.float32)], 0)
        assert a.shape == (nt, 128, e), (n, a.shape)
        out[n] = a.reshape(8, nt // 8 * 128, e)
    return out


def make_in_maps(inp):
    x = np.asarray(inp["x"], np.float32)
    wgs = _weight_groups(inp)
    maps = []
    for c in range(NCORES):
        ti = _tok_index(c)
        xl = np.concatenate([x[0, ti], x[1, ti]], axis=0)
        m = {"xT": np.ascontiguousarray(xl.T), "cst": _consts(c, inp), "rope": _rope_tab(c)}
        for n, _, _ in WG:
            m["ws_" + n] = np.ascontiguousarray(wgs[n][c])
        maps.append(m)
    return maps


def assemble(results):
    out = np.empty((2, SEQ, D), np.float32)
    for c in range(NCORES):
        ti = _tok_index(c)
        o = np.asarray(results[c]["outT"]).T
        out[0, ti] = o[:1024]
        out[1, ti] = o[1024:]
    return out


def kernel(**inputs):
    nc = Kern(99).build()
    res = run_bass_kernel_spmd(nc, make_in_maps(inputs), core_ids=list(range(NCORES)))
    return assemble(res.results)
```
